# Optimizing a Trainium2 kernel written in Bass

```python
import jax, jax.numpy as jnp
from jax import lax
import numpy as np

D_MODEL = 1024
BATCH = 2
SEQ = 8192
DEPTH = 2

N_MIXERS = 2
N_GLA = (DEPTH + 1) // 2
N_MLA = DEPTH // 2
EPS = 1e-6

GLA_HEADS = 4
GLA_DK = D_MODEL // 2 // GLA_HEADS
GLA_DV = D_MODEL // GLA_HEADS
GLA_GATE_RANK = 16
GLA_TAU = 16.0
GLA_CHUNK = 64

MLA_HEADS = 8
MLA_NOPE = 128
MLA_ROPE = 64
MLA_VDIM = 128
MLA_Q_RANK = 384
MLA_KV_RANK = 256
ROPE_THETA = 10000.0
Q_BLOCK = 128

D_FF = -(-8 * D_MODEL // (3 * 256)) * 256

kernel_name = "hybrid_gla_mla_adaln_trunk"


def rmsnorm(t, g):
    tf = t.astype(jnp.float32)
    y = tf * lax.rsqrt(jnp.mean(tf * tf, axis=-1, keepdims=True) + EPS)
    return (y * g.astype(jnp.float32)).astype(t.dtype)


def modulate(t, g, shift, scale):
    return rmsnorm(t, g) * (1.0 + scale[:, None, :]) + shift[:, None, :]


def gla_chunked(q, k, v, log_a):
    B, S, H, DK = q.shape
    DV = v.shape[-1]
    C = GLA_CHUNK
    n = S // C

    def to_chunks(t):
        return t.astype(jnp.float32).reshape(B, n, C, H, t.shape[-1]).transpose(1, 0, 3, 2, 4)

    qc, kc, vc, gc = to_chunks(q), to_chunks(k), to_chunks(v), to_chunks(log_a)
    causal = jnp.tril(jnp.ones((C, C), dtype=bool))[:, :, None]

    def step(state, inp):
        qi, ki, vi, gi = inp
        b = jnp.cumsum(gi, axis=-2)
        o_inter = jnp.einsum('bhck,bhkv->bhcv', qi * jnp.exp(b), state)
        diff = b[:, :, :, None, :] - b[:, :, None, :, :]
        decay = jnp.exp(jnp.where(causal, diff, -jnp.inf))
        attn = jnp.einsum('bhik,bhjk,bhijk->bhij', qi, ki, decay)
        o_intra = jnp.einsum('bhij,bhjv->bhiv', attn, vi)
        b_last = b[:, :, -1:, :]
        new_state = state * jnp.exp(b_last[:, :, 0, :])[..., None] + jnp.einsum(
            'bhck,bhcv->bhkv', ki * jnp.exp(b_last - b), vi)
        return new_state, o_inter + o_intra

    state0 = jnp.zeros((B, H, DK, DV), jnp.float32)
    _, o = lax.scan(step, state0, (qc, kc, vc, gc))
    return o.transpose(1, 0, 3, 2, 4).reshape(B, S, H, DV)


def gla_mixer(h, w_in, w_gate, b_gate, g_out, w_out):
    B, S, _ = h.shape
    H, DK, DV = GLA_HEADS, GLA_DK, GLA_DV
    proj = h @ w_in
    q, k, v, r, glr = jnp.split(proj, [H * DK, 2 * H * DK, 2 * H * DK + H * DV,
                                       2 * H * DK + 2 * H * DV], axis=-1)
    q = q.reshape(B, S, H, DK) * (DK ** -0.5)
    k = k.reshape(B, S, H, DK)
    v = v.reshape(B, S, H, DV)
    r = r.reshape(B, S, H, DV)
    log_a = jax.nn.log_sigmoid((glr @ w_gate + b_gate).astype(jnp.float32)) / GLA_TAU
    o = gla_chunked(q, k, v, log_a.reshape(B, S, H, DK)).astype(h.dtype)
    o = rmsnorm(o, g_out) * jax.nn.silu(r)
    return o.reshape(B, S, H * DV) @ w_out


def apply_rope(t, cos, sin):
    tf = t.astype(jnp.float32).reshape(*t.shape[:-1], -1, 2)
    t1, t2 = tf[..., 0], tf[..., 1]
    out = jnp.stack([t1 * cos - t2 * sin, t1 * sin + t2 * cos], axis=-1).reshape(t.shape)
    return out.astype(t.dtype)


def mla_mixer(h, cos, sin, w_in, g_q, w_q_up, g_kv, w_kv_up, w_out):
    B, S, _ = h.shape
    H = MLA_HEADS
    proj = h @ w_in
    cq, ckv, k_rope = jnp.split(proj, [MLA_Q_RANK, MLA_Q_RANK + MLA_KV_RANK], axis=-1)
    q = (rmsnorm(cq, g_q) @ w_q_up).reshape(B, S, H, MLA_NOPE + MLA_ROPE)
    q_nope, q_rope = jnp.split(q, [MLA_NOPE], axis=-1)
    kv = (rmsnorm(ckv, g_kv) @ w_kv_up).reshape(B, S, H, MLA_NOPE + MLA_VDIM)
    k_nope, v = jnp.split(kv, [MLA_NOPE], axis=-1)
    q_rope = apply_rope(q_rope, cos, sin)
    k_rope = apply_rope(k_rope[:, :, None, :], cos, sin)[:, :, 0, :]

    scale = (MLA_NOPE + MLA_ROPE) ** -0.5
    nb = S // Q_BLOCK
    qn = q_nope.reshape(B, nb, Q_BLOCK, H, MLA_NOPE).transpose(1, 0, 2, 3, 4)
    qr = q_rope.reshape(B, nb, Q_BLOCK, H, MLA_ROPE).transpose(1, 0, 2, 3, 4)
    kpos = jnp.arange(S)

    def block(args):
        qn_b, qr_b, idx = args
        s = (jnp.einsum('bqhd,bkhd->bhqk', qn_b, k_nope)
             + jnp.einsum('bqhd,bkd->bhqk', qr_b, k_rope)).astype(jnp.float32) * scale
        qpos = idx * Q_BLOCK + jnp.arange(Q_BLOCK)
        mask = kpos[None, :] <= qpos[:, None]
        p = jax.nn.softmax(jnp.where(mask, s, -jnp.inf), axis=-1)
        return jnp.einsum('bhqk,bkhd->bqhd', p.astype(v.dtype), v)

    o = lax.map(block, (qn, qr, jnp.arange(nb)))
    o = o.transpose(1, 0, 2, 3, 4).reshape(B, S, H * MLA_VDIM)
    return o @ w_out


def swiglu(h, w_in, w_out):
    gate, up = jnp.split(h @ w_in, 2, axis=-1)
    return (jax.nn.silu(gate) * up) @ w_out


def setup_inputs(seed: int = 0) -> dict:
    key = jax.random.key(seed)
    ks = jax.random.split(key, 24)
    D = D_MODEL
    f32 = jnp.float32

    def w(k, shape, fan_in):
        return jax.random.normal(k, shape, f32) * (fan_in ** -0.5)

    def gain(k, shape):
        return 1.0 + 0.02 * jax.random.normal(k, shape, f32)

    x = jax.random.normal(ks[0], (BATCH, SEQ, D), f32)
    c = jax.random.normal(ks[1], (BATCH, D), f32)
    offsets = jax.random.randint(ks[2], (BATCH, 1), 0, 4096, dtype=jnp.int32)
    positions = offsets + jnp.arange(SEQ, dtype=jnp.int32)[None, :]

    gla_in = 2 * GLA_HEADS * GLA_DK + 2 * GLA_HEADS * GLA_DV + GLA_GATE_RANK
    mla_in = MLA_Q_RANK + MLA_KV_RANK + MLA_ROPE
    return {
        "x": x,
        "c": c,
        "positions": positions,
        "ada_w": w(ks[3], (DEPTH, D, 6 * D), D),
        "ada_b": 0.02 * jax.random.normal(ks[4], (DEPTH, 6 * D), f32),
        "norm_mix": gain(ks[5], (DEPTH, D)),
        "norm_ffn": gain(ks[6], (DEPTH, D)),
        "gla_w_in": w(ks[7], (N_GLA, D, gla_in), D),
        "gla_w_gate": w(ks[8], (N_GLA, GLA_GATE_RANK, GLA_HEADS * GLA_DK), GLA_GATE_RANK),
        "gla_b_gate": 0.1 * jax.random.normal(ks[9], (N_GLA, GLA_HEADS * GLA_DK), f32),
        "gla_g_out": gain(ks[10], (N_GLA, GLA_DV)),
        "gla_w_out": w(ks[11], (N_GLA, GLA_HEADS * GLA_DV, D), GLA_HEADS * GLA_DV),
        "mla_w_in": w(ks[12], (N_MLA, D, mla_in), D),
        "mla_g_q": gain(ks[13], (N_MLA, MLA_Q_RANK)),
        "mla_w_q_up": w(ks[14], (N_MLA, MLA_Q_RANK, MLA_HEADS * (MLA_NOPE + MLA_ROPE)), MLA_Q_RANK),
        "mla_g_kv": gain(ks[15], (N_MLA, MLA_KV_RANK)),
        "mla_w_kv_up": w(ks[16], (N_MLA, MLA_KV_RANK, MLA_HEADS * (MLA_NOPE + MLA_VDIM)), MLA_KV_RANK),
        "mla_w_out": w(ks[17], (N_MLA, MLA_HEADS * MLA_VDIM, D), MLA_HEADS * MLA_VDIM),
        "ffn_w_in": w(ks[18], (DEPTH, D, 2 * D_FF), D),
        "ffn_w_out": w(ks[19], (DEPTH, D_FF, D), D_FF),
        "final_norm": gain(ks[20], (D,)),
    }


def reference(x, c, positions, ada_w, ada_b, norm_mix, norm_ffn,
              gla_w_in, gla_w_gate, gla_b_gate, gla_g_out, gla_w_out,
              mla_w_in, mla_g_q, mla_w_q_up, mla_g_kv, mla_w_kv_up, mla_w_out,
              ffn_w_in, ffn_w_out, final_norm):
    inv_freq = ROPE_THETA ** (-jnp.arange(0, MLA_ROPE, 2, dtype=jnp.float32) / MLA_ROPE)
    ang = positions.astype(jnp.float32)[..., None] * inv_freq
    cos = jnp.cos(ang)[:, :, None, :]
    sin = jnp.sin(ang)[:, :, None, :]
    c_act = jax.nn.silu(c)

    for i in range(DEPTH):
        mod = c_act @ ada_w[i] + ada_b[i]
        sh1, sc1, g1, sh2, sc2, g2 = jnp.split(mod, 6, axis=-1)
        h = modulate(x, norm_mix[i], sh1, sc1)
        j = i // N_MIXERS
        if i % N_MIXERS == 0:
            y = gla_mixer(h, gla_w_in[j], gla_w_gate[j], gla_b_gate[j], gla_g_out[j], gla_w_out[j])
        else:
            y = mla_mixer(h, cos, sin, mla_w_in[j], mla_g_q[j], mla_w_q_up[j],
                          mla_g_kv[j], mla_w_kv_up[j], mla_w_out[j])
        x = x + g1[:, None, :] * y
        h = modulate(x, norm_ffn[i], sh2, sc2)
        x = x + g2[:, None, :] * swiglu(h, ffn_w_in[i], ffn_w_out[i])

    return rmsnorm(x, final_norm)
```

```python
import numpy as np
from contextlib import ExitStack
import concourse.bass as bass
import concourse.mybir as mybir
from concourse.bass_utils import run_bass_kernel_spmd

F32 = mybir.dt.float32
BF16 = mybir.dt.bfloat16
I32 = mybir.dt.int32
AF = mybir.ActivationFunctionType
ALU = mybir.AluOpType
AX = mybir.AxisListType

SAME_ENG_SYNC = True
SCHEDULE = True
CRIT_PRIO = True
import os as _os
GLAF = set(_os.environ.get('GLAF', '').split(','))
STRICT_SYNC = True
EPS = 1e-6
D = 1024
DFF = 2816
NTOK = 2048
SEQ = 8192


class Buf:
    __slots__ = ("name", "last_w", "readers")

    def __init__(self, name=""):
        self.name = name
        self.last_w = None
        self.readers = []


class Op:
    __slots__ = ("eng", "fn", "deps", "is_dma", "semkey", "sig", "cnt", "name", "inc", "sem",
                 "odeps", "cost", "idx", "fin", "users", "nwait", "bl")


class Prog:
    ENGS = ("pe", "act", "dve", "pool", "sp")
    NDMA_SEMS = 44
    NSW_SEMS = 26

    def __init__(self, nc):
        self.nc = nc
        self.ges = ExitStack()
        self.eng_sem = None
        self.dma_pool = None
        self.sem_cnt = {}
        self.n_sb = 0
        self.phase = 0
        self.stats_all = []
        self._reset()

    def _reset(self):
        self.es = ExitStack()
        self.ops = {e: [] for e in self.ENGS}
        self.all_ops = []
        self.dma_keys = {}

    def sb(self, shape, dt, name=None):
        self.n_sb += 1
        return self.es.enter_context(self.nc.sbuf_tensor(f"s{self.phase}_" + (name or f"sb{self.n_sb}"), list(shape), dt))

    def ps(self, shape, dt, name=None):
        self.n_sb += 1
        return self.es.enter_context(self.nc.psum_tensor(f"p{self.phase}_" + (name or f"ps{self.n_sb}"), list(shape), dt))

    def op(self, eng, fn, reads=(), writes=(), dma_key=None, name="", inc=16, cost=0.2):
        o = Op()
        o.cost = cost
        o.idx = len(self.all_ops)
        o.eng = eng
        o.fn = fn
        o.is_dma = dma_key is not None
        o.semkey = dma_key
        o.sig = False
        o.cnt = None
        o.name = name
        o.inc = inc
        o.sem = None
        cand = []
        for b in reads:
            if b.last_w is not None:
                cand.append((b.last_w, "raw"))
        for b in writes:
            if b.last_w is not None:
                cand.append((b.last_w, "waw"))
            for r in b.readers:
                cand.append((r, "war"))
        keep = []
        o.odeps = []
        for d, kind in cand:
            if d is o:
                continue
            if d not in o.odeps:
                o.odeps.append(d)
            if d in keep:
                continue
            if (not d.is_dma) and d.eng == eng:
                if eng == "pe" or not SAME_ENG_SYNC or (kind != "raw" and not STRICT_SYNC):
                    continue
            keep.append(d)
            d.sig = True
        o.deps = keep
        for b in reads:
            b.readers.append(o)
        for b in writes:
            b.last_w = o
            b.readers = []
        self.ops[eng].append(o)
        self.all_ops.append(o)
        if o.is_dma:
            self.dma_keys.setdefault(dma_key, 0)
        return o

    def emit(self):
        nc = self.nc
        if self.eng_sem is None:
            self.eng_sem = {e: self.ges.enter_context(nc.semaphore(f"sem_{e}")) for e in self.ENGS}
            self.dma_pool = [self.ges.enter_context(nc.semaphore(f"dsem_{i}")) for i in range(self.NDMA_SEMS)]
            self.sw_pool = [self.ges.enter_context(nc.semaphore(f"swsem_{i}")) for i in range(self.NSW_SEMS)]
            for s in list(self.eng_sem.values()) + self.dma_pool + self.sw_pool:
                self.sem_cnt[id(s)] = 0
        if SCHEDULE:
            self._schedule()
        key_eng = {}
        for o in self.all_ops:
            if o.is_dma:
                key_eng.setdefault(o.semkey, o.eng)
        dma_sem = {}
        npool = 0
        nsw = 0
        for kname in self.dma_keys:
            if kname.startswith("cc"):
                s = self.ges.enter_context(nc.semaphore(f"ccsem_{kname}"))
                self.sem_cnt[id(s)] = 0
                dma_sem[kname] = s
            elif key_eng[kname] == "pool":
                dma_sem[kname] = self.sw_pool[nsw]
                nsw += 1
            else:
                dma_sem[kname] = self.dma_pool[npool]
                npool += 1
        cnt = self.sem_cnt
        for o in self.all_ops:
            if o.is_dma:
                o.sem = dma_sem[o.semkey]
                cnt[id(o.sem)] += o.inc
                o.cnt = cnt[id(o.sem)]
            else:
                o.sem = self.eng_sem[o.eng]
                if o.sig:
                    cnt[id(o.sem)] += 1
                    o.cnt = cnt[id(o.sem)]
        used = list(dma_sem.values())

        def mk(e):
            def body(engobj):
                seen = {}
                for o in self.ops[e]:
                    for d in o.deps:
                        key = id(d.sem)
                        if seen.get(key, 0) >= d.cnt:
                            continue
                        engobj.wait_ge(d.sem, d.cnt)
                        seen[key] = d.cnt
                    inst = o.fn(engobj)
                    if o.is_dma:
                        if o.inc == 1:
                            inst.then_inc(o.sem)
                        else:
                            inst.then_inc(o.sem, o.inc)
                    elif o.sig:
                        inst.then_inc(o.sem, 1)
                if e == "sp":
                    for s in used:
                        if cnt[id(s)] > 0:
                            engobj.wait_ge(s, cnt[id(s)])
                    for e2 in self.ENGS:
                        s = self.eng_sem[e2]
                        if cnt[id(s)] > 0 and e2 != "sp":
                            engobj.wait_ge(s, cnt[id(s)])
            return body

        with nc.Block() as block:
            block.tensor(mk("pe"))
            block.scalar(mk("act"))
            block.vector(mk("dve"))
            block.gpsimd(mk("pool"))
            block.sync(mk("sp"))
        st = {e: len(v) for e, v in self.ops.items()}
        st["sems"] = len(dma_sem) + 5
        st["maxcnt"] = max(cnt.values())
        self.stats_all.append(st)
        self.stats = st
        self.es.close()
        self.phase += 1
        self._reset()

    def _schedule(self):
        import heapq
        ops = self.all_ops
        for o in ops:
            o.users = []
            o.fin = None
        for o in ops:
            o.nwait = len(o.odeps)
            for d in o.odeps:
                d.users.append(o)
        HOP = 0.35
        for o in reversed(ops):
            o.bl = o.cost + max([u.bl + HOP for u in o.users], default=0.0)
        t_eng = {e: 0.0 for e in self.ENGS}
        ready = {e: [] for e in self.ENGS}
        for o in ops:
            if o.nwait == 0:
                heapq.heappush(ready[o.eng], (0.0, o.idx, o))
        new_ops = {e: [] for e in self.ENGS}
        order = []
        n_left = len(ops)
        while n_left:
            best = None
            for e in self.ENGS:
                h = ready[e]
                if not h:
                    continue
                te = t_eng[e]
                cand = None
                startable = [x for x in h if x[0] <= te]
                if startable:
                    x = max(startable, key=lambda x: (x[2].bl, -x[1])) if CRIT_PRIO else min(startable, key=lambda x: x[1])
                    cand = (te, x[1], x)
                else:
                    x = h[0]
                    cand = (x[0], x[1], x)
                if best is None or cand[:2] < best[0][:2]:
                    best = (cand, e)
            (start, _, x), e = best
            ready[e].remove(x)
            heapq.heapify(ready[e])
            o = x[2]
            if o.is_dma:
                t_eng[e] = start + 0.4
                o.fin = start + o.cost
            else:
                t_eng[e] = start + o.cost
                o.fin = t_eng[e]
            new_ops[e].append(o)
            order.append(o)
            n_left -= 1
            for u in o.users:
                u.nwait -= 1
                if u.nwait == 0:
                    rt = max(d.fin + (0.0 if (d.eng == u.eng and not d.is_dma) else HOP) for d in u.odeps)
                    heapq.heappush(ready[u.eng], (rt, u.idx, u))
        self.ops = new_ops
        self.all_ops = order
        self.sim_time = max(t_eng.values())

    def close(self):
        self.es.close()
        self.ges.close()


class TT:
    __slots__ = ("t", "b")

    def __init__(self, t, name=""):
        self.t = t
        self.b = Buf(name)


class Rot:
    def __init__(self, items):
        self.items = items
        self.i = 0

    def next(self):
        it = self.items[self.i % len(self.items)]
        self.i += 1
        return it


class KB:
    def __init__(self, nc):
        self.nc = nc
        self.P = Prog(nc)
        self.nkey = 0

    def din(self, name, shape, dt=F32):
        return self.nc.dram_tensor(name, list(shape), dt, kind="ExternalInput").ap()

    def dout(self, name, shape, dt=F32):
        return self.nc.dram_tensor(name, list(shape), dt, kind="ExternalOutput").ap()

    def sb(self, shape, dt, name):
        return TT(self.P.sb(shape, dt, name), name)

    def ps(self, shape, dt, name):
        return TT(self.P.ps(shape, dt, name), name)

    def rot(self, n, shape, dt, name, psum=False):
        mk = self.ps if psum else self.sb
        return Rot([mk(shape, dt, f"{name}{i}") for i in range(n)])

    @staticmethod
    def nfree(ap):
        n = 1
        for s in list(ap.shape)[1:]:
            n *= int(s)
        return n

    def ecost(self, ap):
        return 0.12 + self.nfree(ap) / 1100.0

    def dma(self, q, out, in_, reads=(), writes=(), key=None, **kw):
        if key is None:
            key = writes[0].b.name if writes else reads[0].b.name
        nb = self.nfree(out) * int(list(out.shape)[0]) * 2
        return self.P.op(q, lambda e: e.dma_start(out=out, in_=in_, **kw), [t.b for t in reads],
                         [t.b for t in writes], dma_key=key, cost=2.0 + nb / 100e3)

    def load(self, dst, src, q="sp", **kw):
        return self.dma(q, dst.t[:], src, writes=[dst], **kw)

    def mm(self, outT, out, lhsT, rhs, start, stop, reads):
        return self.P.op("pe", lambda e: e.matmul(out, lhsT, rhs, start=start, stop=stop),
                         [t.b for t in reads], [outT.b], cost=0.09 + self.nfree(rhs) / 2000.0)

    def tr(self, outT, out, in_, ident, reads):
        return self.P.op("pe", lambda e: e.transpose(out=out, in_=in_, identity=ident),
                         [t.b for t in reads], [outT.b], cost=0.15)

    def act(self, out, in_, func, reads, writes, eng="act", **kw):
        return self.P.op(eng, lambda e: e.activation(out=out, in_=in_, func=func, **kw),
                         [t.b for t in reads], [t.b for t in writes], cost=self.ecost(out))

    def tt(self, eng, out, in0, in1, op, reads, writes):
        return self.P.op(eng, lambda e: e.tensor_tensor(out=out, in0=in0, in1=in1, op=op),
                         [t.b for t in reads], [t.b for t in writes], cost=self.ecost(out))

    def ts(self, eng, out, in0, s1, s2, op0, op1, reads, writes):
        if s2 is None:
            return self.P.op(eng, lambda e: e.tensor_scalar(out=out, in0=in0, scalar1=s1, scalar2=None, op0=op0),
                             [t.b for t in reads], [t.b for t in writes], cost=self.ecost(out))
        return self.P.op(eng, lambda e: e.tensor_scalar(out=out, in0=in0, scalar1=s1, scalar2=s2, op0=op0, op1=op1),
                         [t.b for t in reads], [t.b for t in writes], cost=self.ecost(out))

    def stt(self, out, in0, scalar, in1, op0, op1, reads, writes):
        return self.P.op("dve", lambda e: e.scalar_tensor_tensor(out=out, in0=in0, scalar=scalar, in1=in1,
                                                                 op0=op0, op1=op1),
                         [t.b for t in reads], [t.b for t in writes], cost=self.ecost(out))

    def cp(self, eng, out, in_, reads, writes):
        if eng == "act":
            return self.P.op("act", lambda e: e.copy(out=out, in_=in_), [t.b for t in reads], [t.b for t in writes], cost=self.ecost(out))
        return self.P.op(eng, lambda e: e.tensor_copy(out=out, in_=in_), [t.b for t in reads],
                         [t.b for t in writes], cost=self.ecost(out))

    def recip(self, out, in_, reads, writes):
        return self.P.op("dve", lambda e: e.reciprocal(out=out, in_=in_), [t.b for t in reads],
                         [t.b for t in writes], cost=self.ecost(out))

    def memset(self, eng, out, val, writes):
        return self.P.op(eng, lambda e: e.memset(out, val), [], [t.b for t in writes])

    def end_phase(self):
        self.P.emit()

    def dint(self, name, shape, dt=F32):
        return self.nc.dram_tensor(name, list(shape), dt)

    def finish(self):
        if self.P.all_ops:
            self.P.emit()
        self.stats = self.P.stats_all
        self.P.close()
        return self.nc


class Common:
    def __init__(self, kb, cst_ap, npst=2, norm=True):
        self.kb = kb
        k = kb
        self.cst = k.sb([128, 384], F32, "cst")
        k.load(self.cst, cst_ap)
        self.identb = k.sb([128, 128], BF16, "identb")
        k.cp("dve", self.identb.t[:], self.cst.t[:, 0:128], [self.cst], [self.identb])
        self.small = k.sb([128, 8], F32, "smallc")
        k.memset("dve", self.small.t[:, 0:1], -0.5, [self.small])
        k.memset("dve", self.small.t[:, 1:2], EPS, [self.small])
        k.memset("dve", self.small.t[:, 2:3], 1.0, [self.small])
        k.memset("dve", self.small.t[:, 3:4], 0.0, [self.small])
        self.ident = self.cst.t[:, 0:128]
        self.tri = self.cst.t[:, 128:256]
        self.ones = self.cst.t[:, 256:384]
        if norm:
            self.junk = k.rot(3, [128, 1024], BF16, "njunk")
            self.ss = k.rot(6, [128, 4], F32, "nss")
            self.xn = k.rot(3, [128, 1024], BF16, "nxn")
        self.pst = k.rot(npst, [128, 1024], BF16, "npst", psum=True) if npst else None
        self.evac_i = 0

    def rstd_col(self, src_ap, srcT, n, ssT=None):
        k = self.kb
        s = ssT or self.ss.next()
        j = self.junk.next()
        k.act(j.t[:, 0:n], src_ap, AF.Square, [srcT], [j, s], accum_out=s.t[:, 0:1])
        k.ts("dve", s.t[:, 1:2], s.t[:, 0:1], 1.0 / n, EPS, ALU.mult, ALU.add, [s], [s])
        k.act(s.t[:, 3:4], s.t[:, 1:2], AF.Ln, [s], [s])
        k.act(s.t[:, 2:3], s.t[:, 3:4], AF.Exp, [s], [s], scale=-0.5)
        return s, s.t[:, 2:3]

    def norm_hT(self, xT, x_ap, modT, a_ap, b_ap, hT, hT_fn):
        k = self.kb
        s, rs = self.rstd_col(x_ap, xT, 1024)
        xn = self.xn.next()
        k.ts("dve", xn.t[:], x_ap, rs, None, ALU.mult, None, [xT, s], [xn])
        pst = self.pst.next()
        for j in range(8):
            k.tr(pst, pst.t[:, j * 128:(j + 1) * 128], xn.t[:, j * 128:(j + 1) * 128], self.identb.t[:],
                 [xn, self.identb])
        for j in range(8):
            self.evac_i += 1
            if self.evac_i % 2 == 0:
                k.act(hT_fn(j), pst.t[:, j * 128:(j + 1) * 128], AF.Identity, [pst, modT], [hT],
                      scale=a_ap[:, j:j + 1], bias=b_ap[:, j:j + 1])
            else:
                k.ts("dve", hT_fn(j), pst.t[:, j * 128:(j + 1) * 128], a_ap[:, j:j + 1], b_ap[:, j:j + 1],
                     ALU.mult, ALU.add, [pst, modT], [hT])


def bcast_rows(k, cm, srcT, src_cols_fn, dstT, dst_ap_fn, psT):
    for j in range(8):
        dg = cm.diag.next()
        k.ts("dve", dg.t[:], cm.ident, src_cols_fn(j), None, ALU.mult, None, [cm.cst, srcT], [dg])
        k.mm(psT, psT.t[:, (j % 4) * 128:(j % 4 + 1) * 128], cm.ones, dg.t[:], True, True, [cm.cst, dg])
        k.cp("dve", dst_ap_fn(j), psT.t[:, (j % 4) * 128:(j % 4 + 1) * 128], [psT], [dstT])


class Tail:
    def __init__(self, k, cm, modv, l, nffn, pbank, lean=False):
        self.k, self.cm, self.l, self.pbank = k, cm, l, pbank
        cm.diag = k.rot(2, [128, 128], F32, "diag")
        self.wpool = k.rot(13 if lean else 14, [128, 1024], BF16, "wslab")
        self.gupool = k.rot(4 if lean else 6, [128, 8, 256], BF16, "guslab")
        self.tmp = k.rot(3 if lean else 4, [128, 512], F32, "tmpf")
        self.h2T_r = k.rot(2, [128, 8, 512], BF16, "h2T")
        self.actT_r = k.rot(2, [128, 11, 512], BF16, "actT")
        self.xb_r = k.rot(8, [128, 1024], F32, "xblk")
        self.oT_r = k.rot(4, [128, 8, 512], BF16, "oTs")
        ab = k.sb([128, 16], F32, "ab2")
        k.ts("dve", ab.t[:, 0:8], modv.t[:, l * 48 + 32: l * 48 + 40], 1.0, None, ALU.add, None, [modv], [ab])
        k.tt("dve", ab.t[:, 0:8], ab.t[:, 0:8], nffn.t[:, l * 8:(l + 1) * 8], ALU.mult, [ab, nffn], [ab])
        k.cp("dve", ab.t[:, 8:16], modv.t[:, l * 48 + 24: l * 48 + 32], [modv], [ab])
        self.ab = ab
        self.g1bc = k.sb([128, 1024], F32, "g1bc")
        self.g2bc = k.sb([128, 1024], F32, "g2bc")
        for s, g in ((2, self.g1bc), (5, self.g2bc)):
            psg = pbank.next()
            bcast_rows(k, cm, modv, lambda j, s=s: modv.t[:, l * 48 + s * 8 + j: l * 48 + s * 8 + j + 1], g,
                       lambda j, g=g: g.t[:, j * 128:(j + 1) * 128], psg)

    def half(self, xbs, oTs, w_out_ap, ffn_in_ap, ffn_out_ap):
        k, cm, pbank, tmp = self.k, self.cm, self.pbank, self.tmp
        wo = []
        for kk in range(8):
            w = self.wpool.next()
            k.dma("pool", w.t[:], w_out_ap[kk * 128:(kk + 1) * 128, :], writes=[w])
            wo.append(w)
        for blk in range(8):
            xb = xbs[blk]
            oT = oTs[blk // 4]
            bs = slice((blk % 4) * 128, (blk % 4 + 1) * 128)
            for n in range(2):
                ps = pbank.next()
                for kk in range(8):
                    k.mm(ps, ps.t[:], oT.t[:, kk, bs], wo[kk].t[:, n * 512:(n + 1) * 512],
                         kk == 0, kk == 7, [oT, wo[kk]])
                t = tmp.next()
                k.tt("dve", t.t[:], ps.t[:], self.g1bc.t[:, n * 512:(n + 1) * 512], ALU.mult, [ps, self.g1bc], [t])
                k.tt("dve", xb.t[:, n * 512:(n + 1) * 512], xb.t[:, n * 512:(n + 1) * 512], t.t[:],
                     ALU.add, [xb, t], [xb])
        ab = self.ab
        h2 = [self.h2T_r.next(), self.h2T_r.next()]
        for b8 in range(8):
            hT = h2[b8 // 4]
            cm.norm_hT(xbs[b8], xbs[b8].t[:], ab, ab.t[:, 0:8], ab.t[:, 8:16], hT,
                       lambda j, b8=b8, hT=hT: hT.t[:, j, (b8 % 4) * 128:(b8 % 4 + 1) * 128])
        gs = us = None
        for ffh in range(2):
            aT = [self.actT_r.next(), self.actT_r.next()]
            for s in range(11):
                m = ffh * 11 + s
                if m % 2 == 0:
                    gs = self.gupool.next()
                    us = self.gupool.next()
                    c0 = m * 128
                    k.dma("pool", gs.t[:], ffn_in_ap[:, c0:c0 + 256].rearrange("(k p) n -> p k n", p=128),
                          writes=[gs])
                    k.dma("pool", us.t[:], ffn_in_ap[:, DFF + c0:DFF + c0 + 256].rearrange("(k p) n -> p k n", p=128),
                          writes=[us])
                c = m % 2
                for tq in range(2):
                    pg = pbank.next()
                    pu = pbank.next()
                    for kk in range(8):
                        k.mm(pg, pg.t[:], gs.t[:, kk, c * 128:(c + 1) * 128], h2[tq].t[:, kk, :],
                             kk == 0, kk == 7, [gs, h2[tq]])
                    for kk in range(8):
                        k.mm(pu, pu.t[:], us.t[:, kk, c * 128:(c + 1) * 128], h2[tq].t[:, kk, :],
                             kk == 0, kk == 7, [us, h2[tq]])
                    t = tmp.next()
                    k.act(t.t[:], pg.t[:], AF.Silu, [pg], [t])
                    k.tt("dve", aT[tq].t[:, s, :], t.t[:], pu.t[:], ALU.mult, [t, pu], [aT[tq]])
            wf = []
            for s in range(11):
                m = ffh * 11 + s
                w = self.wpool.next()
                k.dma("pool", w.t[:], ffn_out_ap[m * 128:(m + 1) * 128, :], writes=[w])
                wf.append(w)
            for b8 in range(8):
                xb = xbs[b8]
                bs = slice((b8 % 4) * 128, (b8 % 4 + 1) * 128)
                for n in range(2):
                    ps = pbank.next()
                    for s in range(11):
                        k.mm(ps, ps.t[:], aT[b8 // 4].t[:, s, bs], wf[s].t[:, n * 512:(n + 1) * 512],
                             s == 0, s == 10, [aT[b8 // 4], wf[s]])
                    t = tmp.next()
                    k.tt("dve", t.t[:], ps.t[:], self.g2bc.t[:, n * 512:(n + 1) * 512], ALU.mult, [ps, self.g2bc], [t])
                    k.tt("dve", xb.t[:, n * 512:(n + 1) * 512], xb.t[:, n * 512:(n + 1) * 512], t.t[:],
                         ALU.add, [xb, t], [xb])


def mod_phase(k, cm, cT_ap, ada_w_ap, adab_ap, pbank):
    cT = k.sb([128, 8], F32, "cT")
    k.load(cT, cT_ap)
    cact = k.sb([128, 8], BF16, "cact")
    k.act(cact.t[:], cT.t[:], AF.Silu, [cT], [cact])
    adab = k.sb([128, 96], F32, "adab")
    k.load(adab, adab_ap)
    modv = k.sb([128, 96], F32, "modv")
    modvA = k.sb([128, 16], F32, "modvA")
    slabs = k.rot(3, [128, 8, 512], BF16, "adaslab")
    ps = pbank.next()
    for l in range(2):
        for cs in range(12):
            s = slabs.next()
            k.dma("pool", s.t[:], ada_w_ap[l, :, cs * 512:(cs + 1) * 512].rearrange("(k p) n -> p k n", p=128),
                  writes=[s])
            for jj in range(4):
                j = cs * 4 + jj
                for kk in range(8):
                    k.mm(ps, ps.t[:, l * 48 + j: l * 48 + j + 1], s.t[:, kk, jj * 128:(jj + 1) * 128],
                         cact.t[:, kk:kk + 1], kk == 0, kk == 7, [s, cact])
            if l == 0 and cs == 3:
                k.tt("dve", modvA.t[:], ps.t[:, 0:16], adab.t[:, 0:16], ALU.add, [ps, adab], [modvA])
    k.tt("dve", modv.t[:], ps.t[:, 0:96], adab.t[:], ALU.add, [ps, adab], [modv])
    return modv, modvA


def make_pbank(k, n=6):
    return k.rot(n, [128, 512], F32, "bank", psum=True)


def phase_mod(k, cst, cT, ada_w, adab, out):
    cm = Common(k, cst)
    pbank = make_pbank(k)
    modv, _ = mod_phase(k, cm, cT, ada_w, adab, pbank)
    k.dma("sp", out, modv.t[:], reads=[modv])
    k.end_phase()


class FrontInTail:
    def __init__(self, k, cm, pbank, modv, nmix_in, w_in, gq_in, gkv_in, lat_out, cc):
        self.k, self.cm, self.pbank, self.lat_out, self.cc = k, cm, pbank, lat_out, cc
        nmix = k.sb([128, 16], F32, "nmixf"); k.load(nmix, nmix_in)
        self.gq = k.sb([128, 384], F32, "gq"); k.load(self.gq, gq_in)
        self.gkv = k.sb([128, 256], F32, "gkv"); k.load(self.gkv, gkv_in)
        self.W = k.sb([128, 8, 768], BF16, "Wfront")
        k.dma("pool", self.W.t[:], w_in.rearrange("(k p) n -> p k n", p=128), writes=[self.W])
        ab = k.sb([128, 16], F32, "ab1f")
        k.ts("dve", ab.t[:, 0:8], modv.t[:, 48 + 8:48 + 16], 1.0, None, ALU.add, None, [modv], [ab])
        k.tt("dve", ab.t[:, 0:8], ab.t[:, 0:8], nmix.t[:, 8:16], ALU.mult, [ab, nmix], [ab])
        k.cp("dve", ab.t[:, 8:16], modv.t[:, 48:56], [modv], [ab])
        self.ab = ab
        self.hT_r = k.rot(2, [128, 8, 128], BF16, "hTf")
        self.lat_r = k.rot(2, [128, 768], BF16, "lat")
        self.latT_r = k.rot(2, [128, 6, 128], BF16, "latTs")
        self.l1T = None

    def block(self, xb, blk):
        k, cm, ab = self.k, self.cm, self.ab
        hT = self.hT_r.next()
        cm.norm_hT(xb, xb.t[:], ab, ab.t[:, 0:8], ab.t[:, 8:16], hT, lambda j: hT.t[:, j, :])
        lat = self.lat_r.next()
        p1 = self.pbank.next()
        for kk in range(8):
            k.mm(p1, p1.t[:], hT.t[:, kk, :], self.W.t[:, kk, 0:512], kk == 0, kk == 7, [hT, self.W])
        s1, r1 = cm.rstd_col(p1.t[:, 0:384], p1, 384)
        k.stt(lat.t[:, 0:384], p1.t[:, 0:384], r1, self.gq.t[:], ALU.mult, ALU.mult, [p1, s1, self.gq], [lat])
        k.cp("act", lat.t[:, 384:512], p1.t[:, 384:512], [p1], [lat])
        p2 = self.pbank.next()
        for kk in range(8):
            k.mm(p2, p2.t[:, 0:256], hT.t[:, kk, :], self.W.t[:, kk, 512:768], kk == 0, kk == 7, [hT, self.W])
        s2, r2 = cm.rstd_col(p2.t[:, 0:256], p2, 256)
        k.stt(lat.t[:, 512:768], p2.t[:, 0:256], r2, self.gkv.t[:], ALU.mult, ALU.mult, [p2, s2, self.gkv], [lat])
        pl = cm.pst.next()
        for c in range(6):
            k.tr(pl, pl.t[:, c * 128:(c + 1) * 128], lat.t[:, c * 128:(c + 1) * 128], cm.identb.t[:],
                 [lat, cm.identb])
        lT = self.latT_r.next()
        k.cp("act", lT.t[:], pl.t[:, 0:768].rearrange("p (c t) -> p c t", c=6), [pl], [lT])
        hf = blk // 8
        if blk % 8 == 0:
            self.l1T = [TT(None, f"l1d{c3}_{hf}") for c3 in range(3)]
        tl = (blk % 8) * 128
        for c3 in range(3):
            k.dma("sp", self.lat_out[c3][hf][:, tl:tl + 128].rearrange("(c p) t -> p c t", p=128),
                  lT.t[:, 2 * c3:2 * c3 + 2, :], reads=[lT], writes=[self.l1T[c3]])
        if self.cc is not None and blk % 8 == 7:
            for c3 in range(3):
                cc_allgather(k, self.cc[c3][hf][0], self.cc[c3][hf][1], self.l1T[c3], f"cc2_{c3}_{hf}")


def phase_tail(k, l, final, cst, x_in, G, segmask_in, modv_in, nffn_in, w_out, ffn_in, ffn_out, fn_in, x_out, front=None):
    cm = Common(k, cst)
    pbank = make_pbank(k)
    modv = k.sb([128, 96], F32, "modv")
    k.load(modv, modv_in)
    nffn = k.sb([128, 16], F32, "nffn")
    k.load(nffn, nffn_in)
    segm = k.sb([128, 4], F32, "segm")
    k.load(segm, segmask_in)
    if final:
        fnb = k.sb([128, D], F32, "fnb")
        k.load(fnb, fn_in)
    tail = Tail(k, cm, modv, l, nffn, pbank, lean=front is not None)
    fr = FrontInTail(k, cm, pbank, modv, *front) if front is not None else None
    stg_r = k.rot(2, [128, 8, 256 if front is not None else 512], BF16, "ostg")
    for half in range(2):
        t0 = half * 1024
        xbs = [tail.xb_r.next() for _ in range(8)]
        for blk in range(8):
            k.dma("sp", xbs[blk].t[:], x_in[t0 + blk * 128:t0 + (blk + 1) * 128, :], writes=[xbs[blk]])
        oTs = [tail.oT_r.next(), tail.oT_r.next()]
        SW = 256 if front is not None else 512
        for sub in range(2):
            oT = oTs[sub]
            for piece in range(512 // SW):
                osl = slice(piece * SW, (piece + 1) * SW)
                for s in range(4):
                    stg = stg_r.next()
                    c0 = t0 + sub * 512 + piece * SW
                    G(stg, s, c0, SW)
                    if s == 0:
                        k.ts("dve", oT.t[:, :, osl], stg.t[:], segm.t[:, 0:1], None, ALU.mult, None, [stg, segm], [oT])
                    else:
                        k.stt(oT.t[:, :, osl], stg.t[:], segm.t[:, s:s + 1], oT.t[:, :, osl], ALU.mult, ALU.add,
                              [stg, segm, oT], [oT])
        tail.half(xbs, oTs, w_out, ffn_in, ffn_out)
        for blk in range(8):
            xb = xbs[blk]
            if final:
                s, rs = cm.rstd_col(xb.t[:], xb, 1024)
                k.stt(xb.t[:], xb.t[:], rs, fnb.t[:], ALU.mult, ALU.mult, [xb, s, fnb], [xb])
            k.dma("sp", x_out[t0 + blk * 128:t0 + (blk + 1) * 128, :], xb.t[:], reads=[xb])
            if fr is not None:
                fr.block(xb, half * 8 + blk)
    k.end_phase()


def fm(v, n):
    return np.ascontiguousarray(np.asarray(v, np.float32).reshape(n, 128).T)


def consts():
    c = np.zeros((128, 384), np.float32)
    c[:, 0:128] = np.eye(128, dtype=np.float32)
    c[:, 128:256] = np.triu(np.ones((128, 128), np.float32))
    c[:, 256:384] = 1.0
    return c


def run(nc, in_maps):
    res = run_bass_kernel_spmd(nc, in_maps, core_ids=list(range(len(in_maps))))
    return res.results


def phase_gla(k, cst, x_in, modv_in, nmix_in, w_in, wg_in, bg_in, gout_in, og_out, ngroups=SEQ // 512, cc=None,
              mod_args=None):
    cm = Common(k, cst, npst=1)
    OGT = [TT(None, f"ogd{q}") for q in range(4)]
    pp = k.rot(3 if mod_args else 4, [128, 512], F32, "gpp", psum=True)
    modvA = None
    if mod_args:
        mps = k.rot(1, [128, 512], F32, "modps", psum=True)
        modv_full, modvA = mod_phase(k, cm, mod_args[0], mod_args[1], mod_args[2], mps)
        k.dma("sp", mod_args[3], modv_full.t[:], reads=[modv_full])
    po = [k.ps([128, 512], F32, "po0"), k.ps([128, 512], F32, "po1")]
    pkd = k.ps([128, 512], BF16, "pkd")
    if modvA is None:
        modvA = k.sb([128, 96], F32, "modvg"); k.load(modvA, modv_in)
    modv = modvA
    nmix = k.sb([128, 16], F32, "nmix"); k.load(nmix, nmix_in)
    wg = k.sb([16, 128], F32, "wg"); k.load(wg, wg_in)
    bg = k.sb([1, 128], F32, "bg"); k.load(bg, bg_in)
    gout = k.sb([128, 2], F32, "gout"); k.load(gout, gout_in)
    W = k.sb([128, 8, 784], BF16, "Wgla")
    k.dma("pool", W.t[:], w_in.rearrange("(k p) n -> p k n", p=128), writes=[W])
    ab = k.sb([128, 16], F32, "ab1")
    k.ts("dve", ab.t[:, 0:8], modv.t[:, 8:16], 1.0, None, ALU.add, None, [modv], [ab])
    k.tt("dve", ab.t[:, 0:8], ab.t[:, 0:8], nmix.t[:, 0:8], ALU.mult, [ab, nmix], [ab])
    k.cp("dve", ab.t[:, 8:16], modv.t[:, 0:8], [modv], [ab])
    S = k.sb([128, 256], F32, "S")
    Sb = k.sb([128, 256], BF16, "Sb")
    k.memset("dve", S.t[:], 0.0, [S])
    k.memset("dve", Sb.t[:], 0.0, [Sb])
    ND = 3
    xg_r = k.rot(2, [128, 4, 1024], F32, "xg")
    hT_r = k.rot(2, [128, 8, 512], BF16, "hT")
    qs_r = k.rot(2, [128, 512], F32, "qs")
    ks_r = k.rot(2, [128, 512], F32, "ks")
    glr_r = k.rot(2, [16, 512], F32, "glrT")
    la_r = k.rot(2, [128, 512], F32, "la")
    E1_r = k.rot(ND, [128, 512], F32, "E1")
    E2_r = k.rot(2, [128, 512], F32, "E2")
    qe_r = k.rot(ND, [128, 512], BF16, "qeT")
    ke_r = k.rot(ND, [128, 512], BF16, "keT")
    kdT_r = k.rot(4, [128, 128], BF16, "kdT")
    kd_r = k.rot(ND, [128, 512], BF16, "kd")
    v_r = k.rot(ND, [128, 4, 256], BF16, "vtok")
    e_r = k.rot(2, [128, 2, 512], F32, "er")
    rg_r = k.rot(2, [128, 2, 512], F32, "rg")
    t2_r = k.rot(ND, [128, 2, 512], F32, "t2")
    at_r = k.rot(3, [128, 128], BF16, "attnT")
    os_r = k.rot(2, [128, 2, 512], F32, "osb")
    sq_r = k.rot(2, [128, 2, 512], F32, "sq")
    ms_r = k.rot(2, [128, 512], F32, "ms")
    t1_r = k.rot(2, [128, 512], F32, "t1")
    og_r = k.rot(2, [128, 2, 512], BF16, "og")
    SC = 128 ** -0.5
    mhalf = k.sb([128, 512], F32, "mhalf")
    k.memset("dve", mhalf.t[:], -0.5, [mhalf])
    for g in range(ngroups):
        t0 = g * 512
        xg = xg_r.next()
        k.dma("sp", xg.t[:], x_in[t0:t0 + 512, :].rearrange("(b p) n -> p b n", p=128), writes=[xg])
        hT = hT_r.next()
        for b4 in range(4):
            cm.norm_hT(xg, xg.t[:, b4, :], ab, ab.t[:, 0:8], ab.t[:, 8:16], hT,
                       lambda j, b4=b4: hT.t[:, j, b4 * 128:(b4 + 1) * 128])
        pq = pp.next()
        for kk in range(8):
            k.mm(pq, pq.t[:], W.t[:, kk, 0:128], hT.t[:, kk, :], kk == 0, kk == 7, [W, hT])
        qs = qs_r.next()
        k.cp("act", qs.t[:], pq.t[:], [pq], [qs])
        pk = pp.next()
        for kk in range(8):
            k.mm(pk, pk.t[:], W.t[:, kk, 128:256], hT.t[:, kk, :], kk == 0, kk == 7, [W, hT])
        ks = ks_r.next()
        k.cp("act" if "ks_act" in GLAF else "dve", ks.t[:], pk.t[:], [pk], [ks])
        pg = pp.next()
        for kk in range(8):
            k.mm(pg, pg.t[0:16, :], W.t[:, kk, 768:784], hT.t[:, kk, :], kk == 0, kk == 7, [W, hT])
        glrT = glr_r.next()
        k.cp("act", glrT.t[:], pg.t[0:16, :], [pg], [glrT])
        pxg = pp.next()
        for b4 in range(4):
            k.mm(pxg, pxg.t[:, b4 * 128:(b4 + 1) * 128], glrT.t[:, b4 * 128:(b4 + 1) * 128], wg.t[:], True, False,
                 [glrT, wg])
            k.mm(pxg, pxg.t[:, b4 * 128:(b4 + 1) * 128], cm.cst.t[0:1, 256:384], bg.t[:], False, True,
                 [cm.cst, bg])
        la = la_r.next()
        k.act(la.t[:], pxg.t[:], AF.Exp, [pxg], [la], scale=-1.0)
        k.act(la.t[:], la.t[:], AF.Ln, [la], [la], bias=1.0, scale=1.0)
        pb = pp.next()
        for b4 in range(4):
            k.mm(pb, pb.t[:, b4 * 128:(b4 + 1) * 128], la.t[:, b4 * 128:(b4 + 1) * 128], cm.tri, True, True,
                 [la, cm.cst])
        E1 = E1_r.next()
        E2 = E2_r.next()
        k.act(E1.t[:], pb.t[:], AF.Exp, [pb], [E1], scale=-1.0 / 16)
        k.act(E2.t[:], pb.t[:], AF.Exp, [pb], [E2], scale=1.0 / 16)
        qe = qe_r.next()
        ke = ke_r.next()
        k.stt(qe.t[:], qs.t[:], SC, E1.t[:], ALU.mult, ALU.mult, [qs, E1], [qe])
        k.tt("dve", ke.t[:], ks.t[:], E2.t[:], ALU.mult, [ks, E2], [ke])
        kd = kd_r.next()
        for b4 in range(4):
            kdT = kdT_r.next()
            sl = slice(b4 * 128, (b4 + 1) * 128)
            k.stt(kdT.t[:], ks.t[:, sl], E1.t[:, b4 * 128 + 127:b4 * 128 + 128], E2.t[:, sl], ALU.mult, ALU.mult,
                  [ks, E1, E2], [kdT])
            k.tr(pkd, pkd.t[:, sl], kdT.t[:], cm.identb.t[:], [kdT, cm.identb])
        k.cp("act", kd.t[:], pkd.t[:], [pkd], [kd])
        vt = v_r.next()
        for b2 in range(2):
            pv = pp.next()
            for bb in range(2):
                b4 = b2 * 2 + bb
                for kk in range(8):
                    k.mm(pv, pv.t[:, bb * 256:(bb + 1) * 256], hT.t[:, kk, b4 * 128:(b4 + 1) * 128],
                         W.t[:, kk, 256:512], kk == 0, kk == 7, [hT, W])
            k.cp("act", vt.t[:, b2 * 2:b2 * 2 + 2, :],
                 pv.t[:].rearrange("p (b n) -> p b n", b=2), [pv], [vt])
        er = e_r.next()
        rg = rg_r.next()
        t2 = t2_r.next()
        for c in range(2):
            pr = pp.next()
            for kk in range(8):
                k.mm(pr, pr.t[:], W.t[:, kk, 512 + c * 128:512 + (c + 1) * 128], hT.t[:, kk, :], kk == 0, kk == 7,
                     [W, hT])
            k.act(er.t[:, c, :], pr.t[:], AF.Exp, [pr], [er], scale=-1.0)
            k.act(er.t[:, c, :], er.t[:, c, :], AF.Ln, [er], [er], bias=1.0, scale=1.0)
            k.act(er.t[:, c, :], er.t[:, c, :], AF.Exp, [er], [er], scale=-1.0)
            k.stt(t2.t[:, c, :], pr.t[:], gout.t[:, c:c + 1], er.t[:, c, :], ALU.mult, ALU.mult, [pr, gout, er], [t2])
        for b4 in range(4):
            sl = slice(b4 * 128, (b4 + 1) * 128)
            pa = pp.next()
            k.mm(pa, pa.t[:, 0:128], ke.t[:, sl], qe.t[:, sl], True, True, [ke, qe])
            at = at_r.next()
            k.tt("dve", at.t[:], pa.t[:, 0:128], cm.tri, ALU.mult, [pa, cm.cst], [at])
            for c in range(2):
                k.mm(po[c], po[c].t[:, sl], Sb.t[:, c * 128:(c + 1) * 128], qe.t[:, sl], True, False, [Sb, qe])
                k.mm(po[c], po[c].t[:, sl], vt.t[:, b4, c * 128:(c + 1) * 128], at.t[:], False, True, [vt, at])
            pkv = pp.next()
            k.mm(pkv, pkv.t[:, 0:256], kd.t[:, sl], vt.t[:, b4, :], True, True, [kd, vt])
            k.stt(S.t[:], S.t[:], E1.t[:, b4 * 128 + 127:b4 * 128 + 128], pkv.t[:, 0:256], ALU.mult, ALU.add,
                  [S, E1, pkv], [S])
            k.cp("act", Sb.t[:], S.t[:], [S], [Sb])
        osb = os_r.next()
        sq = sq_r.next()
        for c in range(2):
            if "no_osb" in GLAF:
                k.act(sq.t[:, c, :], po[c].t[:], AF.Square, [po[c]], [sq])
            else:
                k.cp("act", osb.t[:, c, :], po[c].t[:], [po[c]], [osb])
                k.act(sq.t[:, c, :], osb.t[:, c, :], AF.Square, [osb], [sq])
        pss = pp.next()
        for c in range(2):
            k.mm(pss, pss.t[:], cm.ones, sq.t[:, c, :], c == 0, c == 1, [cm.cst, sq])
        ms = ms_r.next()
        k.ts("dve", ms.t[:], pss.t[:], 1.0 / 256, EPS, ALU.mult, ALU.add, [pss], [ms])
        k.act(ms.t[:], ms.t[:], AF.Ln, [ms], [ms])
        k.act(ms.t[:], ms.t[:], AF.Exp, [ms], [ms], scale=-0.5)
        og = og_r.next()
        for c in range(2):
            t1 = t1_r.next()
            if "no_osb" in GLAF:
                k.tt("dve", t1.t[:], po[c].t[:], ms.t[:], ALU.mult, [po[c], ms], [t1])
            else:
                k.tt("dve", t1.t[:], osb.t[:, c, :], ms.t[:], ALU.mult, [osb, ms], [t1])
            k.tt("dve", og.t[:, c, :], t1.t[:], t2.t[:, c, :], ALU.mult, [t1, t2], [og])
        k.dma("sp", og_out[g // 4][:, (g % 4) * 512:(g % 4) * 512 + 512].rearrange("(c p) t -> p c t", p=128), og.t[:],
              reads=[og], writes=[OGT[g // 4]])
        if cc is not None and g % 4 == 3:
            cc_allgather(k, cc[g // 4][0], cc[g // 4][1], OGT[g // 4], f"cc1_{g // 4}")
    k.end_phase()


def phase_front(k, cst, x_in, modv_in, nmix_in, w_in, gq_in, gkv_in, lat_out, cc=None):
    cm = Common(k, cst, npst=2)
    pb1 = k.rot(2, [128, 512], F32, "fb1", psum=True)
    pb2 = k.rot(2, [128, 512], F32, "fb2", psum=True)
    plt = k.rot(2, [128, 1024], BF16, "flt", psum=True)
    modv = k.sb([128, 96], F32, "modv"); k.load(modv, modv_in)
    nmix = k.sb([128, 16], F32, "nmix"); k.load(nmix, nmix_in)
    gq = k.sb([128, 384], F32, "gq"); k.load(gq, gq_in)
    gkv = k.sb([128, 256], F32, "gkv"); k.load(gkv, gkv_in)
    W = k.sb([128, 8, 768], BF16, "Wfront")
    k.dma("pool", W.t[:], w_in.rearrange("(k p) n -> p k n", p=128), writes=[W])
    ab = k.sb([128, 16], F32, "ab1")
    k.ts("dve", ab.t[:, 0:8], modv.t[:, 48 + 8:48 + 16], 1.0, None, ALU.add, None, [modv], [ab])
    k.tt("dve", ab.t[:, 0:8], ab.t[:, 0:8], nmix.t[:, 8:16], ALU.mult, [ab, nmix], [ab])
    k.cp("dve", ab.t[:, 8:16], modv.t[:, 48:56], [modv], [ab])
    xb_r = k.rot(4, [128, 1024], F32, "xb")
    hT_r = k.rot(4, [128, 8, 128], BF16, "hTf")
    lat_r = k.rot(4, [128, 768], BF16, "lat")
    latT_r = k.rot(4, [128, 6, 128], BF16, "latTs")
    for blk in range(NTOK // 128):
        t0 = blk * 128
        xb = xb_r.next()
        k.dma("sp", xb.t[:], x_in[t0:t0 + 128, :], writes=[xb])
        hT = hT_r.next()
        cm.norm_hT(xb, xb.t[:], ab, ab.t[:, 0:8], ab.t[:, 8:16], hT, lambda j: hT.t[:, j, :])
        p1 = pb1.next()
        p2 = pb2.next()
        for kk in range(8):
            k.mm(p1, p1.t[:], hT.t[:, kk, :], W.t[:, kk, 0:512], kk == 0, kk == 7, [hT, W])
        for kk in range(8):
            k.mm(p2, p2.t[:, 0:256], hT.t[:, kk, :], W.t[:, kk, 512:768], kk == 0, kk == 7, [hT, W])
        lat = lat_r.next()
        s1, r1 = cm.rstd_col(p1.t[:, 0:384], p1, 384)
        k.stt(lat.t[:, 0:384], p1.t[:, 0:384], r1, gq.t[:], ALU.mult, ALU.mult, [p1, s1, gq], [lat])
        k.cp("act", lat.t[:, 384:512], p1.t[:, 384:512], [p1], [lat])
        s2, r2 = cm.rstd_col(p2.t[:, 0:256], p2, 256)
        k.stt(lat.t[:, 512:768], p2.t[:, 0:256], r2, gkv.t[:], ALU.mult, ALU.mult, [p2, s2, gkv], [lat])
        pl = plt.next()
        for c in range(6):
            k.tr(pl, pl.t[:, c * 128:(c + 1) * 128], lat.t[:, c * 128:(c + 1) * 128], cm.identb.t[:],
                 [lat, cm.identb])
        lT = latT_r.next()
        k.cp("act", lT.t[:], pl.t[:, 0:768].rearrange("p (c t) -> p c t", c=6), [pl], [lT])
        hf = blk // 8
        if blk % 8 == 0:
            l1T = [TT(None, f"l1d{c3}_{hf}") for c3 in range(3)]
        tl = (blk % 8) * 128
        for c3 in range(3):
            k.dma("sp", lat_out[c3][hf][:, tl:tl + 128].rearrange("(c p) t -> p c t", p=128),
                  lT.t[:, 2 * c3:2 * c3 + 2, :], reads=[lT], writes=[l1T[c3]])
        if cc is not None and blk % 8 == 7:
            for c3 in range(3):
                cc_allgather(k, cc[c3][hf][0], cc[c3][hf][1], l1T[c3], f"cc2_{c3}_{hf}")
    k.end_phase()


TWO_PI = 2.0 * np.pi
C1 = 6.28125
C2 = TWO_PI - 6.28125


def phase_attn(k, cst, latG, wq_in, wkv_in, pos_in, ropec_in, dmask_in, o_out, cc=None):
    def lat(f0, f1, t0, t1):
        s = t0 // NTOK
        c = f0 // 256
        hf = (t0 % NTOK) // 1024
        assert (t1 - 1) // NTOK == s and (f1 - 1) // 256 == c and ((t1 - 1) % NTOK) // 1024 == hf
        tl = t0 - s * NTOK - hf * 1024
        return latG[c][hf][s * 256 + f0 - c * 256:s * 256 + f1 - c * 256, tl:tl + (t1 - t0)]

    cm = Common(k, cst, npst=0, norm=False)
    psS = k.rot(4, [128, 512], F32, "psS", psum=True)
    psO = k.rot(1, [128, 512], F32, "psO", psum=True)
    psL = k.rot(1, [128, 512], F32, "psL", psum=True)
    pj = k.rot(2, [128, 512], F32, "pj", psum=True)
    onesb = k.sb([128, 128], BF16, "onesb")
    k.cp("dve", onesb.t[:], cm.ones, [cm.cst], [onesb])
    ropec = k.sb([64, 2], F32, "ropec"); k.load(ropec, ropec_in)
    dmask = k.sb([128, 4, 512], BF16, "dmask")
    k.dma("pool", dmask.t[:], dmask_in, writes=[dmask])
    wq = k.sb([128, 3, 512], BF16, "wq")
    k.dma("pool", wq.t[:], wq_in.rearrange("(c p) n -> p c n", p=128), writes=[wq])
    wkv = k.sb([128, 2, 512], BF16, "wkv")
    k.dma("pool", wkv.t[:], wkv_in.rearrange("(c p) n -> p c n", p=128), writes=[wkv])
    ckv = k.sb([128, 2, SEQ], BF16, "ckvT")
    ckvr = [[TT(ckv.t, f"ckvT{c}_{r}") for r in range(8)] for c in range(2)]
    for r in range(8):
        for c in range(2):
            k.dma("sp", ckv.t[:, c, r * 1024:(r + 1) * 1024],
                  lat(512 + c * 128, 512 + (c + 1) * 128, r * 1024, (r + 1) * 1024), writes=[ckvr[c][r]])
    cosT = k.sb([64, SEQ], BF16, "cosT")
    sinT = k.sb([64, SEQ], BF16, "sinT")
    krot = k.sb([64, SEQ], BF16, "krot")
    cosR = [TT(cosT.t, f"cosT_{r}") for r in range(8)]
    sinR = [TT(sinT.t, f"sinT_{r}") for r in range(8)]
    krotR = [TT(krot.t, f"krot_{r}") for r in range(8)]
    CH = 1024
    posi_r = k.rot(2, [64, CH], I32, "posi")
    f_r = k.rot(6, [64, CH], F32, "ropef")
    ti_r = k.rot(2, [64, CH], I32, "ropei")
    kr_r = k.rot(2, [64, 2, CH], BF16, "krin")
    for ch in range(SEQ // CH):
        sl = slice(ch * CH, (ch + 1) * CH)
        pi_ = posi_r.next()
        k.dma("sp", pi_.t[:], pos_in[:, sl], writes=[pi_])
        ang = f_r.next()
        k.cp("dve", ang.t[:], pi_.t[:], [pi_], [ang])
        k.ts("dve", ang.t[:], ang.t[:], ropec.t[:, 0:1], None, ALU.mult, None, [ang, ropec], [ang])
        for which, dst in ((0, sinR[ch]), (1, cosR[ch])):
            a2 = ang
            if which == 1:
                a2 = f_r.next()
                k.ts("dve", a2.t[:], ang.t[:], float(np.pi / 2), None, ALU.add, None, [ang], [a2])
            ti = ti_r.next()
            k.ts("dve", ti.t[:], a2.t[:], float(1.0 / TWO_PI), None, ALU.mult, None, [a2], [ti])
            kf = f_r.next()
            k.cp("dve", kf.t[:], ti.t[:], [ti], [kf])
            r = f_r.next()
            k.stt(r.t[:], kf.t[:], -C1, a2.t[:], ALU.mult, ALU.add, [kf, a2], [r])
            k.stt(r.t[:], kf.t[:], -C2, r.t[:], ALU.mult, ALU.add, [kf, r], [r])
            k.ts("dve", r.t[:], r.t[:], float(-np.pi), float(np.pi), ALU.max, ALU.min, [r], [r])
            if which == 0:
                k.act(r.t[:], r.t[:], AF.Sin, [r], [r])
                k.ts("dve", dst.t[:, sl], r.t[:], ropec.t[:, 1:2], None, ALU.mult, None, [r, ropec], [dst])
            else:
                k.act(dst.t[:, sl], r.t[:], AF.Sin, [r], [dst])
        kr = kr_r.next()
        k.dma("sp", kr.t[:], lat(384, 512, sl.start, sl.stop).rearrange("(c p) t -> p c t", p=64), writes=[kr])
        t1 = f_r.next()
        t2 = f_r.next()
        k.tt("dve", t1.t[:], kr.t[:, 0, :], cosT.t[:, sl], ALU.mult, [kr, cosR[ch]], [t1])
        k.tt("dve", t2.t[:], kr.t[:, 1, :], sinT.t[:, sl], ALU.mult, [kr, sinR[ch]], [t2])
        k.tt("dve", krot.t[:, sl], t1.t[:], t2.t[:], ALU.add, [t1, t2], [krotR[ch]])
    KT = k.sb([128, SEQ], BF16, "KT")
    KTR = [TT(KT.t, f"KT_{t}") for t in range(16)]
    V = k.sb([128, 64, 128], BF16, "Vtok")
    VR = [TT(V.t, f"V_{t}") for t in range(16)]
    cq_r = k.rot(2, [128, 3, 512], BF16, "cqT")
    QnT_r = k.rot(2, [128, 512], BF16, "QnT")
    Qrot_r = k.rot(2, [64, 512], BF16, "Qrot")
    qt_r = k.rot(4, [64, 512], F32, "qtmp")
    PT_r = k.rot(5, [128, 512], BF16, "PT")
    rl_r = k.rot(2, [128, 512], F32, "rl")
    ot_r = k.rot(2, [128, 512], BF16, "ot")
    SCALE = float(192 ** -0.5)
    for hh in range(2):
        for t in range(16):
            p = pj.next()
            sl = slice(t * 512, (t + 1) * 512)
            for c in range(2):
                k.mm(p, p.t[:], wkv.t[:, c, hh * 256:hh * 256 + 128], ckv.t[:, c, sl], c == 0, c == 1,
                     [wkv, ckvr[c][t // 2]])
            k.cp("act", KT.t[:, sl], p.t[:], [p], [KTR[t]])
        for t in range(16):
            p = pj.next()
            for b4 in range(4):
                kb = t * 4 + b4
                for c in range(2):
                    k.mm(p, p.t[:, b4 * 128:(b4 + 1) * 128], ckv.t[:, c, kb * 128:(kb + 1) * 128],
                         wkv.t[:, c, hh * 256 + 128:hh * 256 + 256], c == 0, c == 1, [ckvr[c][t // 2], wkv])
            k.cp("dve", V.t[:, t * 4:(t + 1) * 4, :], p.t[:].rearrange("p (b n) -> p b n", b=4), [p], [VR[t]])
        for i in range(16):
            sl = slice(i * 512, (i + 1) * 512)
            cq = cq_r.next()
            k.dma("sp", cq.t[:, 0:2, :], lat(0, 256, sl.start, sl.stop).rearrange("(c p) t -> p c t", p=128), writes=[cq])
            k.dma("sp", cq.t[:, 2, :], lat(256, 384, sl.start, sl.stop), writes=[cq])
            p = pj.next()
            for c in range(3):
                k.mm(p, p.t[:], wq.t[:, c, hh * 256:hh * 256 + 128], cq.t[:, c, :], c == 0, c == 2, [wq, cq])
            QnT = QnT_r.next()
            k.cp("act", QnT.t[:], p.t[:], [p], [QnT])
            p1 = pj.next()
            for c in range(3):
                k.mm(p1, p1.t[0:64, :], wq.t[:, c, hh * 256 + 128:hh * 256 + 192], cq.t[:, c, :], c == 0, c == 2,
                     [wq, cq])
            p2 = pj.next()
            for c in range(3):
                k.mm(p2, p2.t[0:64, :], wq.t[:, c, hh * 256 + 192:hh * 256 + 256], cq.t[:, c, :], c == 0, c == 2,
                     [wq, cq])
            q1 = qt_r.next()
            q2 = qt_r.next()
            k.tt("dve", q1.t[:], p1.t[0:64, :], cosT.t[:, sl], ALU.mult, [p1, cosR[i // 2]], [q1])
            k.tt("dve", q2.t[:], p2.t[0:64, :], sinT.t[:, sl], ALU.mult, [p2, sinR[i // 2]], [q2])
            Qrot = Qrot_r.next()
            k.tt("dve", Qrot.t[:], q1.t[:], q2.t[:], ALU.add, [q1, q2], [Qrot])
            pO = psO.next()
            pL = psL.next()
            nkb = 4 * i + 4
            for kb in range(nkb):
                ks = slice(kb * 128, (kb + 1) * 128)
                pS = psS.next()
                k.mm(pS, pS.t[:], KT.t[:, ks], QnT.t[:], True, False, [KTR[kb // 4], QnT])
                k.mm(pS, pS.t[:], krot.t[:, ks], Qrot.t[:], False, True, [krotR[kb // 8], Qrot])
                PT = PT_r.next()
                k.act(PT.t[:], pS.t[:], AF.Exp, [pS], [PT], scale=SCALE)
                if kb >= 4 * i:
                    k.tt("dve", PT.t[:], PT.t[:], dmask.t[:, kb - 4 * i, :], ALU.mult, [PT, dmask], [PT])
                k.mm(pO, pO.t[:], V.t[:, kb, :], PT.t[:], kb == 0, kb == nkb - 1, [VR[kb // 4], PT])
                k.mm(pL, pL.t[:], onesb.t[:], PT.t[:], kb == 0, kb == nkb - 1, [onesb, PT])
            rl = rl_r.next()
            k.recip(rl.t[:], pL.t[:], [pL], [rl])
            ot = ot_r.next()
            k.tt("dve", ot.t[:], pO.t[:], rl.t[:], ALU.mult, [pO, rl], [ot])
            if i % 4 == 0:
                oaT = TT(None, f"oad{hh}_{i // 4}")
            k.dma("sp", o_out[hh][i // 4][:, (i % 4) * 512:(i % 4) * 512 + 512], ot.t[:], reads=[ot], writes=[oaT])
            if cc is not None and i % 4 == 3:
                cc_allgather(k, cc[hh][i // 4][0], cc[hh][i // 4][1], oaT, f"cc3_{hh}_{i // 4}")
    k.end_phase()


GROUPS = [[0, 1, 2, 3], [4, 5, 6, 7]]


def cc_allgather(k, src_t, dst_t, srcTT, key):
    k.P.op("pool", lambda e: e.collective_compute("AllGather", ALU.bypass, replica_groups=GROUPS,
                                                  ins=[src_t.ap().opt()], outs=[dst_t.ap().opt()]),
           [srcTT.b], [], dma_key=key, inc=1, cost=60.0)


def allgather(k, pairs, key):
    for i, (src_t, dst_t) in enumerate(pairs):
        k.P.op("pool", lambda e, src_t=src_t, dst_t=dst_t: e.collective_compute(
            "AllGather", ALU.bypass, replica_groups=GROUPS, ins=[src_t.ap().opt()], outs=[dst_t.ap().opt()]),
            [], [], dma_key=f"{key}_{i}", inc=1)
    k.end_phase()


def build_fused(upto=99):
    nc = bass.Bass("TRN2", target_bir_lowering=False)
    k = KB(nc)
    cst = k.din("cst", [128, 384]); cT = k.din("cT", [128, 8]); ada_w = k.din("ada_w", [2, D, 6 * D])
    adab = k.din("adab", [128, 96]); nmix = k.din("nmix", [128, 16]); nffn = k.din("nffn", [128, 16])
    segm = k.din("segm", [128, 4]); xfull = k.din("xfull", [SEQ, D]); xseg = k.din("xseg", [NTOK, D])
    gw_in = k.din("gw_in", [D, 784]); gw_gate = k.din("gw_gate", [16, 128]); gb_gate = k.din("gb_gate", [1, 128])
    g_out = k.din("g_out", [128, 2]); gla_w_out = k.din("gla_w_out", [D, D])
    ffn_in0 = k.din("ffn_in0", [D, 2 * DFF]); ffn_out0 = k.din("ffn_out0", [DFF, D])
    wfront = k.din("wfront", [D, 768]); gq = k.din("gq", [128, 384]); gkv = k.din("gkv", [128, 256])
    wq = k.din("wq", [384, 512]); wkv = k.din("wkv", [256, 512]); pos = k.din("pos", [64, SEQ], I32)
    ropec = k.din("ropec", [64, 2]); dmask = k.din("dmask", [128, 4, 512])
    mla_w_out = k.din("mla_w_out", [D, D]); ffn_in1 = k.din("ffn_in1", [D, 2 * DFF])
    ffn_out1 = k.din("ffn_out1", [DFF, D]); fnorm = k.din("fnorm", [128, D])
    out = k.dout("out", [NTOK, D])
    MODV = k.dint("i_modv", [128, 96])
    X0 = k.dint("i_x0", [NTOK, D])
    OG = [k.dint(f"i_og{q}", [256, NTOK], BF16) for q in range(4)]
    G1 = [k.dint(f"i_g1{q}", [1024, NTOK], BF16) for q in range(4)]
    L1 = [[k.dint(f"i_l1{c}_{h}", [256, 1024], BF16) for h in range(2)] for c in range(3)]
    G2 = [[k.dint(f"i_g2{c}_{h}", [1024, 1024], BF16) for h in range(2)] for c in range(3)]
    OA = [[k.dint(f"i_oa{hh}_{q}", [128, NTOK], BF16) for q in range(4)] for hh in range(2)]
    G3 = [[k.dint(f"i_g3{hh}_{q}", [512, NTOK], BF16) for q in range(4)] for hh in range(2)]
    aps = lambda ts: [t.ap() for t in ts]

    def g1_load(stg, s, c0, w=512):
        k.dma("sp", stg.t[:], G1[s].ap()[:, c0:c0 + w].rearrange("(c p) t -> p c t", p=128), writes=[stg])

    def g3_load(stg, s, c0, w=512):
        for hh in range(2):
            k.dma("sp", stg.t[:, hh:8:2, :], G3[hh][s].ap()[:, c0:c0 + w].rearrange("(c p) t -> p c t", p=128),
                  writes=[stg])

    phase_gla(k, cst, xfull, None, nmix, gw_in, gw_gate, gb_gate, g_out, aps(OG), cc=list(zip(OG, G1)),
              mod_args=(cT, ada_w, adab, MODV.ap()))
    if upto <= 2:
        return k.finish(), k
    phase_tail(k, 0, False, cst, xseg, g1_load, segm, MODV.ap(), nffn, gla_w_out, ffn_in0, ffn_out0, None, X0.ap(),
               front=(nmix, wfront, gq, gkv, [aps(L1[c]) for c in range(3)],
                      [[(L1[c][h], G2[c][h]) for h in range(2)] for c in range(3)]))
    if upto <= 3:
        return k.finish(), k
    phase_attn(k, cst, [aps(G2[c]) for c in range(3)], wq, wkv, pos, ropec, dmask, [aps(OA[0]), aps(OA[1])],
               cc=[list(zip(OA[0], G3[0])), list(zip(OA[1], G3[1]))])
    if upto <= 5:
        return k.finish(), k
    phase_tail(k, 1, True, cst, X0.ap(), g3_load, segm, MODV.ap(), nffn, mla_w_out, ffn_in1, ffn_out1, fnorm, out)
    return k.finish(), k
    phase_gla(k, cst, xfull, MODV.ap(), nmix, gw_in, gw_gate, gb_gate, g_out, aps(OG))
    if upto <= 2:
        return k.finish(), k
    allgather(k, list(zip(OG, G1)), "cc1")
    if upto <= 3:
        return k.finish(), k
    phase_tail(k, 0, False, cst, xseg, aps(G1), segm, MODV.ap(), nffn, gla_w_out, ffn_in0, ffn_out0, None, X0.ap())
    if upto <= 4:
        return k.finish(), k
    phase_front(k, cst, X0.ap(), MODV.ap(), nmix, wfront, gq, gkv, aps(L1))
    if upto <= 5:
        return k.finish(), k
    allgather(k, list(zip(L1, G2)), "cc2")
    if upto <= 6:
        return k.finish(), k
    phase_attn(k, cst, aps(G2), wq, wkv, pos, ropec, dmask, aps(OA))
    if upto <= 7:
        return k.finish(), k
    allgather(k, list(zip(OA, G3)), "cc3")
    if upto <= 8:
        return k.finish(), k
    phase_tail(k, 1, True, cst, X0.ap(), aps(G3), segm, MODV.ap(), nffn, mla_w_out, ffn_in1, ffn_out1, fnorm, out)
    if upto <= 9:
        return k.finish(), k
    return k.finish(), k


_CACHE = {}
_PREP_ONLY = False


def _swap_pairs(w):
    idx = np.arange(w.shape[1]).reshape(-1, 2)[:, ::-1].reshape(-1)
    return w[:, idx]


def kernel(x, c, positions, ada_w, ada_b, norm_mix, norm_ffn,
           gla_w_in, gla_w_gate, gla_b_gate, gla_g_out, gla_w_out,
           mla_w_in, mla_g_q, mla_w_q_up, mla_g_kv, mla_w_kv_up, mla_w_out,
           ffn_w_in, ffn_w_out, final_norm):
    f32 = np.float32
    A = lambda v: np.asarray(v, f32)
    x = A(x); c = A(c); positions = np.asarray(positions, np.int32)
    ada_w = A(ada_w); ada_b = A(ada_b)
    cst = consts()
    nmix = np.concatenate([fm(A(norm_mix)[l], 8) for l in range(2)], axis=1)
    nffn = np.concatenate([fm(A(norm_ffn)[l], 8) for l in range(2)], axis=1)
    adab = np.concatenate([fm(ada_b[l], 48) for l in range(2)], axis=1)
    Wi = A(gla_w_in)[0]; wg = A(gla_w_gate)[0]; bgt = A(gla_b_gate)[0]
    gout = fm(A(gla_g_out)[0], 2)
    Wm = A(mla_w_in)[0]
    wfront = np.ascontiguousarray(np.concatenate([Wm[:, 0:384], Wm[:, 640:704], _swap_pairs(Wm[:, 640:704]),
                                                  Wm[:, 384:640]], axis=1))
    gq = np.ascontiguousarray(np.broadcast_to(A(mla_g_q)[0][None, :], (128, 384)))
    gkv = np.ascontiguousarray(np.broadcast_to(A(mla_g_kv)[0][None, :], (128, 256)))
    Wq = A(mla_w_q_up)[0]; Wkv = A(mla_w_kv_up)[0]
    inv_freq = (np.float32(10000.0) ** (-np.arange(0, 64, 2, dtype=np.float32) / np.float32(64))).astype(f32)
    ropec = np.stack([np.repeat(inv_freq, 2), np.tile(np.array([-1.0, 1.0], f32), 32)], axis=1).astype(f32)
    pidx = np.arange(128)[:, None, None] + 128 * np.arange(4)[None, :, None]
    dmask = (pidx <= np.arange(512)[None, None, :]).astype(f32)
    fnb = np.ascontiguousarray(np.broadcast_to(A(final_norm)[None, :], (128, D)))
    ims = []
    for cr in range(8):
        b, j = cr // 4, cr % 4
        w = np.concatenate([Wi[:, j * 128:(j + 1) * 128], Wi[:, 512 + j * 128:512 + (j + 1) * 128],
                            Wi[:, 1024 + j * 256:1024 + (j + 1) * 256], Wi[:, 2048 + j * 256:2048 + (j + 1) * 256],
                            Wi[:, 3072:3088]], axis=1)
        wq_cols, wkv_cols = [], []
        for h in (2 * j, 2 * j + 1):
            wq_cols += [Wq[:, h * 192:h * 192 + 128], Wq[:, h * 192 + 128:h * 192 + 192],
                        _swap_pairs(Wq[:, h * 192 + 128:h * 192 + 192])]
            wkv_cols += [Wkv[:, h * 256:h * 256 + 128], Wkv[:, h * 256 + 128:h * 256 + 256]]
        segm = np.zeros((128, 4), f32)
        segm[:, j] = 1.0
        ims.append({
            "cst": cst, "cT": fm(c[b], 8), "ada_w": ada_w, "adab": adab, "nmix": nmix, "nffn": nffn, "segm": segm,
            "xfull": x[b], "xseg": np.ascontiguousarray(x[b, j * NTOK:(j + 1) * NTOK]),
            "gw_in": np.ascontiguousarray(w), "gw_gate": np.ascontiguousarray(wg[:, j * 128:(j + 1) * 128]),
            "gb_gate": np.ascontiguousarray(bgt[None, j * 128:(j + 1) * 128]), "g_out": gout,
            "gla_w_out": A(gla_w_out)[0], "ffn_in0": A(ffn_w_in)[0], "ffn_out0": A(ffn_w_out)[0],
            "wfront": wfront, "gq": gq, "gkv": gkv,
            "wq": np.ascontiguousarray(np.concatenate(wq_cols, axis=1)),
            "wkv": np.ascontiguousarray(np.concatenate(wkv_cols, axis=1)),
            "pos": np.ascontiguousarray(np.broadcast_to(positions[b][None, :], (64, SEQ))),
            "ropec": ropec, "dmask": dmask, "mla_w_out": A(mla_w_out)[0], "ffn_in1": A(ffn_w_in)[1],
            "ffn_out1": A(ffn_w_out)[1], "fnorm": fnb})
    if _PREP_ONLY:
        return ims
    if "nc" not in _CACHE:
        _CACHE["nc"] = build_fused()[0]
    res = run_bass_kernel_spmd(_CACHE["nc"], ims, core_ids=list(range(8))).results
    out = np.stack([np.concatenate([np.asarray(res[b * 4 + s]["out"]) for s in range(4)], axis=0)
                    for b in range(2)])
    return out.astype(f32)
```

```python
import numpy as np
from contextlib import ExitStack
import concourse.bass as bass
import concourse.mybir as mybir
from concourse.bass_utils import run_bass_kernel_spmd

F32 = mybir.dt.float32
BF16 = mybir.dt.bfloat16
I32 = mybir.dt.int32
AF = mybir.ActivationFunctionType
ALU = mybir.AluOpType
AX = mybir.AxisListType

SAME_ENG_SYNC = True
SCHEDULE = True
CRIT_PRIO = True
import os as _os
GLAF = set(_os.environ.get('GLAF', '').split(','))
STRICT_SYNC = True
EPS = 1e-6
D = 1024
DFF = 2816
NTOK = 2048
SEQ = 8192


class Buf:
    __slots__ = ("name", "last_w", "readers")

    def __init__(self, name=""):
        self.name = name
        self.last_w = None
        self.readers = []


class Op:
    __slots__ = ("eng", "fn", "deps", "is_dma", "semkey", "sig", "cnt", "name", "inc", "sem",
                 "odeps", "cost", "idx", "fin", "users", "nwait", "bl")


class Prog:
    ENGS = ("pe", "act", "dve", "pool", "sp")
    NDMA_SEMS = 44
    NSW_SEMS = 26

    def __init__(self, nc):
        self.nc = nc
        self.ges = ExitStack()
        self.eng_sem = None
        self.dma_pool = None
        self.sem_cnt = {}
        self.n_sb = 0
        self.phase = 0
        self.stats_all = []
        self._reset()

    def _reset(self):
        self.es = ExitStack()
        self.ops = {e: [] for e in self.ENGS}
        self.all_ops = []
        self.dma_keys = {}

    def sb(self, shape, dt, name=None):
        self.n_sb += 1
        return self.es.enter_context(self.nc.sbuf_tensor(f"s{self.phase}_" + (name or f"sb{self.n_sb}"), list(shape), dt))

    def ps(self, shape, dt, name=None):
        self.n_sb += 1
        return self.es.enter_context(self.nc.psum_tensor(f"p{self.phase}_" + (name or f"ps{self.n_sb}"), list(shape), dt))

    def op(self, eng, fn, reads=(), writes=(), dma_key=None, name="", inc=16, cost=0.2):
        o = Op()
        o.cost = cost
        o.idx = len(self.all_ops)
        o.eng = eng
        o.fn = fn
        o.is_dma = dma_key is not None
        o.semkey = dma_key
        o.sig = False
        o.cnt = None
        o.name = name
        o.inc = inc
        o.sem = None
        cand = []
        for b in reads:
            if b.last_w is not None:
                cand.append((b.last_w, "raw"))
        for b in writes:
            if b.last_w is not None:
                cand.append((b.last_w, "waw"))
            for r in b.readers:
                cand.append((r, "war"))
        keep = []
        o.odeps = []
        for d, kind in cand:
            if d is o:
                continue
            if d not in o.odeps:
                o.odeps.append(d)
            if d in keep:
                continue
            if (not d.is_dma) and d.eng == eng:
                if eng == "pe" or not SAME_ENG_SYNC or (kind != "raw" and not STRICT_SYNC):
                    continue
            keep.append(d)
            d.sig = True
        o.deps = keep
        for b in reads:
            b.readers.append(o)
        for b in writes:
            b.last_w = o
            b.readers = []
        self.ops[eng].append(o)
        self.all_ops.append(o)
        if o.is_dma:
            self.dma_keys.setdefault(dma_key, 0)
        return o

    def emit(self):
        nc = self.nc
        if self.eng_sem is None:
            self.eng_sem = {e: self.ges.enter_context(nc.semaphore(f"sem_{e}")) for e in self.ENGS}
            self.dma_pool = [self.ges.enter_context(nc.semaphore(f"dsem_{i}")) for i in range(self.NDMA_SEMS)]
            self.sw_pool = [self.ges.enter_context(nc.semaphore(f"swsem_{i}")) for i in range(self.NSW_SEMS)]
            for s in list(self.eng_sem.values()) + self.dma_pool + self.sw_pool:
                self.sem_cnt[id(s)] = 0
        if SCHEDULE:
            self._schedule()
        key_eng = {}
        for o in self.all_ops:
            if o.is_dma:
                key_eng.setdefault(o.semkey, o.eng)
        dma_sem = {}
        npool = 0
        nsw = 0
        for kname in self.dma_keys:
            if kname.startswith("cc"):
                s = self.ges.enter_context(nc.semaphore(f"ccsem_{kname}"))
                self.sem_cnt[id(s)] = 0
                dma_sem[kname] = s
            elif key_eng[kname] == "pool":
                dma_sem[kname] = self.sw_pool[nsw]
                nsw += 1
            else:
                dma_sem[kname] = self.dma_pool[npool]
                npool += 1
        cnt = self.sem_cnt
        for o in self.all_ops:
            if o.is_dma:
                o.sem = dma_sem[o.semkey]
                cnt[id(o.sem)] += o.inc
                o.cnt = cnt[id(o.sem)]
            else:
                o.sem = self.eng_sem[o.eng]
                if o.sig:
                    cnt[id(o.sem)] += 1
                    o.cnt = cnt[id(o.sem)]
        used = list(dma_sem.values())

        def mk(e):
            def body(engobj):
                seen = {}
                for o in self.ops[e]:
                    for d in o.deps:
                        key = id(d.sem)
                        if seen.get(key, 0) >= d.cnt:
                            continue
                        engobj.wait_ge(d.sem, d.cnt)
                        seen[key] = d.cnt
                    inst = o.fn(engobj)
                    if o.is_dma:
                        if o.inc == 1:
                            inst.then_inc(o.sem)
                        else:
                            inst.then_inc(o.sem, o.inc)
                    elif o.sig:
                        inst.then_inc(o.sem, 1)
                if e == "sp":
                    for s in used:
                        if cnt[id(s)] > 0:
                            engobj.wait_ge(s, cnt[id(s)])
                    for e2 in self.ENGS:
                        s = self.eng_sem[e2]
                        if cnt[id(s)] > 0 and e2 != "sp":
                            engobj.wait_ge(s, cnt[id(s)])
            return body

        with nc.Block() as block:
            block.tensor(mk("pe"))
            block.scalar(mk("act"))
            block.vector(mk("dve"))
            block.gpsimd(mk("pool"))
            block.sync(mk("sp"))
        st = {e: len(v) for e, v in self.ops.items()}
        st["sems"] = len(dma_sem) + 5
        st["maxcnt"] = max(cnt.values())
        self.stats_all.append(st)
        self.stats = st
        self.es.close()
        self.phase += 1
        self._reset()

    def _schedule(self):
        import heapq
        ops = self.all_ops
        for o in ops:
            o.users = []
            o.fin = None
        for o in ops:
            o.nwait = len(o.odeps)
            for d in o.odeps:
                d.users.append(o)
        HOP = 0.35
        for o in reversed(ops):
            o.bl = o.cost + max([u.bl + HOP for u in o.users], default=0.0)
        t_eng = {e: 0.0 for e in self.ENGS}
        ready = {e: [] for e in self.ENGS}
        for o in ops:
            if o.nwait == 0:
                heapq.heappush(ready[o.eng], (0.0, o.idx, o))
        new_ops = {e: [] for e in self.ENGS}
        order = []
        n_left = len(ops)
        while n_left:
            best = None
            for e in self.ENGS:
                h = ready[e]
                if not h:
                    continue
                te = t_eng[e]
                cand = None
                startable = [x for x in h if x[0] <= te]
                if startable:
                    x = max(startable, key=lambda x: (x[2].bl, -x[1])) if CRIT_PRIO else min(startable, key=lambda x: x[1])
                    cand = (te, x[1], x)
                else:
                    x = h[0]
                    cand = (x[0], x[1], x)
                if best is None or cand[:2] < best[0][:2]:
                    best = (cand, e)
            (start, _, x), e = best
            ready[e].remove(x)
            heapq.heapify(ready[e])
            o = x[2]
            if o.is_dma:
                t_eng[e] = start + 0.08
                o.fin = start + o.cost
            else:
                t_eng[e] = start + o.cost
                o.fin = t_eng[e]
            new_ops[e].append(o)
            order.append(o)
            n_left -= 1
            for u in o.users:
                u.nwait -= 1
                if u.nwait == 0:
                    rt = max(d.fin + (0.0 if (d.eng == u.eng and not d.is_dma) else HOP) for d in u.odeps)
                    heapq.heappush(ready[u.eng], (rt, u.idx, u))
        self.ops = new_ops
        self.all_ops = order
        self.sim_time = max(t_eng.values())

    def close(self):
        self.es.close()
        self.ges.close()


class TT:
    __slots__ = ("t", "b")

    def __init__(self, t, name=""):
        self.t = t
        self.b = Buf(name)


class Rot:
    def __init__(self, items):
        self.items = items
        self.i = 0

    def next(self):
        it = self.items[self.i % len(self.items)]
        self.i += 1
        return it


class KB:
    def __init__(self, nc):
        self.nc = nc
        self.P = Prog(nc)
        self.nkey = 0

    def din(self, name, shape, dt=F32):
        return self.nc.dram_tensor(name, list(shape), dt, kind="ExternalInput").ap()

    def dout(self, name, shape, dt=F32):
        return self.nc.dram_tensor(name, list(shape), dt, kind="ExternalOutput").ap()

    def sb(self, shape, dt, name):
        return TT(self.P.sb(shape, dt, name), name)

    def ps(self, shape, dt, name):
        return TT(self.P.ps(shape, dt, name), name)

    def rot(self, n, shape, dt, name, psum=False):
        mk = self.ps if psum else self.sb
        return Rot([mk(shape, dt, f"{name}{i}") for i in range(n)])

    @staticmethod
    def nfree(ap):
        n = 1
        for s in list(ap.shape)[1:]:
            n *= int(s)
        return n

    def ecost(self, ap):
        return 0.12 + self.nfree(ap) / 1100.0

    def dma(self, q, out, in_, reads=(), writes=(), key=None, **kw):
        if key is None:
            key = writes[0].b.name if writes else reads[0].b.name
        nb = self.nfree(out) * int(list(out.shape)[0]) * 2
        return self.P.op(q, lambda e: e.dma_start(out=out, in_=in_, **kw), [t.b for t in reads],
                         [t.b for t in writes], dma_key=key, cost=2.0 + nb / 100e3)

    def load(self, dst, src, q="sp", **kw):
        return self.dma(q, dst.t[:], src, writes=[dst], **kw)

    def mm(self, outT, out, lhsT, rhs, start, stop, reads):
        return self.P.op("pe", lambda e: e.matmul(out, lhsT, rhs, start=start, stop=stop),
                         [t.b for t in reads], [outT.b], cost=0.09 + self.nfree(rhs) / 2000.0)

    def tr(self, outT, out, in_, ident, reads):
        return self.P.op("pe", lambda e: e.transpose(out=out, in_=in_, identity=ident),
                         [t.b for t in reads], [outT.b], cost=0.15)

    def act(self, out, in_, func, reads, writes, eng="act", **kw):
        return self.P.op(eng, lambda e: e.activation(out=out, in_=in_, func=func, **kw),
                         [t.b for t in reads], [t.b for t in writes], cost=self.ecost(out))

    def tt(self, eng, out, in0, in1, op, reads, writes):
        return self.P.op(eng, lambda e: e.tensor_tensor(out=out, in0=in0, in1=in1, op=op),
                         [t.b for t in reads], [t.b for t in writes], cost=self.ecost(out))

    def ts(self, eng, out, in0, s1, s2, op0, op1, reads, writes):
        if s2 is None:
            return self.P.op(eng, lambda e: e.tensor_scalar(out=out, in0=in0, scalar1=s1, scalar2=None, op0=op0),
                             [t.b for t in reads], [t.b for t in writes], cost=self.ecost(out))
        return self.P.op(eng, lambda e: e.tensor_scalar(out=out, in0=in0, scalar1=s1, scalar2=s2, op0=op0, op1=op1),
                         [t.b for t in reads], [t.b for t in writes], cost=self.ecost(out))

    def stt(self, out, in0, scalar, in1, op0, op1, reads, writes):
        return self.P.op("dve", lambda e: e.scalar_tensor_tensor(out=out, in0=in0, scalar=scalar, in1=in1,
                                                                 op0=op0, op1=op1),
                         [t.b for t in reads], [t.b for t in writes], cost=self.ecost(out))

    def cp(self, eng, out, in_, reads, writes):
        if eng == "act":
            return self.P.op("act", lambda e: e.copy(out=out, in_=in_), [t.b for t in reads], [t.b for t in writes], cost=self.ecost(out))
        return self.P.op(eng, lambda e: e.tensor_copy(out=out, in_=in_), [t.b for t in reads],
                         [t.b for t in writes], cost=self.ecost(out))

    def recip(self, out, in_, reads, writes):
        return self.P.op("dve", lambda e: e.reciprocal(out=out, in_=in_), [t.b for t in reads],
                         [t.b for t in writes], cost=self.ecost(out))

    def memset(self, eng, out, val, writes):
        return self.P.op(eng, lambda e: e.memset(out, val), [], [t.b for t in writes])

    def end_phase(self):
        self.P.emit()

    def dint(self, name, shape, dt=F32):
        return self.nc.dram_tensor(name, list(shape), dt)

    def finish(self):
        if self.P.all_ops:
            self.P.emit()
        self.stats = self.P.stats_all
        self.P.close()
        return self.nc


class Common:
    def __init__(self, kb, cst_ap, npst=2, norm=True):
        self.kb = kb
        k = kb
        self.cst = k.sb([128, 384], F32, "cst")
        k.load(self.cst, cst_ap)
        self.identb = k.sb([128, 128], BF16, "identb")
        k.cp("dve", self.identb.t[:], self.cst.t[:, 0:128], [self.cst], [self.identb])
        self.small = k.sb([128, 8], F32, "smallc")
        k.memset("dve", self.small.t[:, 0:1], -0.5, [self.small])
        k.memset("dve", self.small.t[:, 1:2], EPS, [self.small])
        k.memset("dve", self.small.t[:, 2:3], 1.0, [self.small])
        k.memset("dve", self.small.t[:, 3:4], 0.0, [self.small])
        self.ident = self.cst.t[:, 0:128]
        self.tri = self.cst.t[:, 128:256]
        self.ones = self.cst.t[:, 256:384]
        if norm:
            self.junk = k.rot(3, [128, 1024], BF16, "njunk")
            self.ss = k.rot(6, [128, 4], F32, "nss")
            self.xn = k.rot(3, [128, 1024], BF16, "nxn")
        self.pst = k.rot(npst, [128, 1024], BF16, "npst", psum=True) if npst else None
        self.evac_i = 0

    def rstd_col(self, src_ap, srcT, n, ssT=None):
        k = self.kb
        s = ssT or self.ss.next()
        j = self.junk.next()
        k.act(j.t[:, 0:n], src_ap, AF.Square, [srcT], [j, s], accum_out=s.t[:, 0:1])
        k.ts("dve", s.t[:, 1:2], s.t[:, 0:1], 1.0 / n, EPS, ALU.mult, ALU.add, [s], [s])
        k.act(s.t[:, 3:4], s.t[:, 1:2], AF.Ln, [s], [s])
        k.act(s.t[:, 2:3], s.t[:, 3:4], AF.Exp, [s], [s], scale=-0.5)
        return s, s.t[:, 2:3]

    def norm_hT(self, xT, x_ap, modT, a_ap, b_ap, hT, hT_fn):
        k = self.kb
        s, rs = self.rstd_col(x_ap, xT, 1024)
        xn = self.xn.next()
        k.ts("dve", xn.t[:], x_ap, rs, None, ALU.mult, None, [xT, s], [xn])
        pst = self.pst.next()
        for j in range(8):
            k.tr(pst, pst.t[:, j * 128:(j + 1) * 128], xn.t[:, j * 128:(j + 1) * 128], self.identb.t[:],
                 [xn, self.identb])
        for j in range(8):
            self.evac_i += 1
            if self.evac_i % 2 == 0:
                k.act(hT_fn(j), pst.t[:, j * 128:(j + 1) * 128], AF.Identity, [pst, modT], [hT],
                      scale=a_ap[:, j:j + 1], bias=b_ap[:, j:j + 1])
            else:
                k.ts("dve", hT_fn(j), pst.t[:, j * 128:(j + 1) * 128], a_ap[:, j:j + 1], b_ap[:, j:j + 1],
                     ALU.mult, ALU.add, [pst, modT], [hT])


def bcast_rows(k, cm, srcT, src_cols_fn, dstT, dst_ap_fn, psT):
    for j in range(8):
        dg = cm.diag.next()
        k.ts("dve", dg.t[:], cm.ident, src_cols_fn(j), None, ALU.mult, None, [cm.cst, srcT], [dg])
        k.mm(psT, psT.t[:, (j % 4) * 128:(j % 4 + 1) * 128], cm.ones, dg.t[:], True, True, [cm.cst, dg])
        k.cp("dve", dst_ap_fn(j), psT.t[:, (j % 4) * 128:(j % 4 + 1) * 128], [psT], [dstT])


class Tail:
    def __init__(self, k, cm, modv, l, nffn, pbank, lean=False):
        self.k, self.cm, self.l, self.pbank = k, cm, l, pbank
        cm.diag = k.rot(2, [128, 128], F32, "diag")
        self.wpool = k.rot(13 if lean else 14, [128, 1024], BF16, "wslab")
        self.gupool = k.rot(4 if lean else 6, [128, 8, 256], BF16, "guslab")
        self.tmp = k.rot(3 if lean else 4, [128, 512], F32, "tmpf")
        self.h2T_r = k.rot(2, [128, 8, 512], BF16, "h2T")
        self.actT_r = k.rot(2, [128, 11, 512], BF16, "actT")
        self.xb_r = k.rot(8, [128, 1024], F32, "xblk")
        self.oT_r = k.rot(4, [128, 8, 512], BF16, "oTs")
        ab = k.sb([128, 16], F32, "ab2")
        k.ts("dve", ab.t[:, 0:8], modv.t[:, l * 48 + 32: l * 48 + 40], 1.0, None, ALU.add, None, [modv], [ab])
        k.tt("dve", ab.t[:, 0:8], ab.t[:, 0:8], nffn.t[:, l * 8:(l + 1) * 8], ALU.mult, [ab, nffn], [ab])
        k.cp("dve", ab.t[:, 8:16], modv.t[:, l * 48 + 24: l * 48 + 32], [modv], [ab])
        self.ab = ab
        self.g1bc = k.sb([128, 1024], F32, "g1bc")
        self.g2bc = k.sb([128, 1024], F32, "g2bc")
        for s, g in ((2, self.g1bc), (5, self.g2bc)):
            psg = pbank.next()
            bcast_rows(k, cm, modv, lambda j, s=s: modv.t[:, l * 48 + s * 8 + j: l * 48 + s * 8 + j + 1], g,
                       lambda j, g=g: g.t[:, j * 128:(j + 1) * 128], psg)

    def half(self, xbs, oTs, w_out_ap, ffn_in_ap, ffn_out_ap):
        k, cm, pbank, tmp = self.k, self.cm, self.pbank, self.tmp
        wo = []
        for kk in range(8):
            w = self.wpool.next()
            k.dma("pool", w.t[:], w_out_ap[kk * 128:(kk + 1) * 128, :], writes=[w])
            wo.append(w)
        for blk in range(8):
            xb = xbs[blk]
            oT = oTs[blk // 4]
            bs = slice((blk % 4) * 128, (blk % 4 + 1) * 128)
            for n in range(2):
                ps = pbank.next()
                for kk in range(8):
                    k.mm(ps, ps.t[:], oT.t[:, kk, bs], wo[kk].t[:, n * 512:(n + 1) * 512],
                         kk == 0, kk == 7, [oT, wo[kk]])
                t = tmp.next()
                k.tt("dve", t.t[:], ps.t[:], self.g1bc.t[:, n * 512:(n + 1) * 512], ALU.mult, [ps, self.g1bc], [t])
                k.tt("dve", xb.t[:, n * 512:(n + 1) * 512], xb.t[:, n * 512:(n + 1) * 512], t.t[:],
                     ALU.add, [xb, t], [xb])
        ab = self.ab
        h2 = [self.h2T_r.next(), self.h2T_r.next()]
        for b8 in range(8):
            hT = h2[b8 // 4]
            cm.norm_hT(xbs[b8], xbs[b8].t[:], ab, ab.t[:, 0:8], ab.t[:, 8:16], hT,
                       lambda j, b8=b8, hT=hT: hT.t[:, j, (b8 % 4) * 128:(b8 % 4 + 1) * 128])
        gs = us = None
        for ffh in range(2):
            aT = [self.actT_r.next(), self.actT_r.next()]
            for s in range(11):
                m = ffh * 11 + s
                if m % 2 == 0:
                    gs = self.gupool.next()
                    us = self.gupool.next()
                    c0 = m * 128
                    k.dma("pool", gs.t[:], ffn_in_ap[:, c0:c0 + 256].rearrange("(k p) n -> p k n", p=128),
                          writes=[gs])
                    k.dma("pool", us.t[:], ffn_in_ap[:, DFF + c0:DFF + c0 + 256].rearrange("(k p) n -> p k n", p=128),
                          writes=[us])
                c = m % 2
                for tq in range(2):
                    pg = pbank.next()
                    pu = pbank.next()
                    for kk in range(8):
                        k.mm(pg, pg.t[:], gs.t[:, kk, c * 128:(c + 1) * 128], h2[tq].t[:, kk, :],
                             kk == 0, kk == 7, [gs, h2[tq]])
                    for kk in range(8):
                        k.mm(pu, pu.t[:], us.t[:, kk, c * 128:(c + 1) * 128], h2[tq].t[:, kk, :],
                             kk == 0, kk == 7, [us, h2[tq]])
                    t = tmp.next()
                    k.act(t.t[:], pg.t[:], AF.Silu, [pg], [t])
                    k.tt("dve", aT[tq].t[:, s, :], t.t[:], pu.t[:], ALU.mult, [t, pu], [aT[tq]])
            wf = []
            for s in range(11):
                m = ffh * 11 + s
                w = self.wpool.next()
                k.dma("pool", w.t[:], ffn_out_ap[m * 128:(m + 1) * 128, :], writes=[w])
                wf.append(w)
            for b8 in range(8):
                xb = xbs[b8]
                bs = slice((b8 % 4) * 128, (b8 % 4 + 1) * 128)
                for n in range(2):
                    ps = pbank.next()
                    for s in range(11):
                        k.mm(ps, ps.t[:], aT[b8 // 4].t[:, s, bs], wf[s].t[:, n * 512:(n + 1) * 512],
                             s == 0, s == 10, [aT[b8 // 4], wf[s]])
                    t = tmp.next()
                    k.tt("dve", t.t[:], ps.t[:], self.g2bc.t[:, n * 512:(n + 1) * 512], ALU.mult, [ps, self.g2bc], [t])
                    k.tt("dve", xb.t[:, n * 512:(n + 1) * 512], xb.t[:, n * 512:(n + 1) * 512], t.t[:],
                         ALU.add, [xb, t], [xb])


def mod_phase(k, cm, cT_ap, ada_w_ap, adab_ap, pbank):
    cT = k.sb([128, 8], F32, "cT")
    k.load(cT, cT_ap)
    cact = k.sb([128, 8], BF16, "cact")
    k.act(cact.t[:], cT.t[:], AF.Silu, [cT], [cact])
    adab = k.sb([128, 96], F32, "adab")
    k.load(adab, adab_ap)
    modv = k.sb([128, 96], F32, "modv")
    modvA = k.sb([128, 16], F32, "modvA")
    slabs = k.rot(3, [128, 8, 512], BF16, "adaslab")
    ps = pbank.next()
    for l in range(2):
        for cs in range(12):
            s = slabs.next()
            k.dma("pool", s.t[:], ada_w_ap[l, :, cs * 512:(cs + 1) * 512].rearrange("(k p) n -> p k n", p=128),
                  writes=[s])
            for jj in range(4):
                j = cs * 4 + jj
                for kk in range(8):
                    k.mm(ps, ps.t[:, l * 48 + j: l * 48 + j + 1], s.t[:, kk, jj * 128:(jj + 1) * 128],
                         cact.t[:, kk:kk + 1], kk == 0, kk == 7, [s, cact])
            if l == 0 and cs == 3:
                k.tt("dve", modvA.t[:], ps.t[:, 0:16], adab.t[:, 0:16], ALU.add, [ps, adab], [modvA])
    k.tt("dve", modv.t[:], ps.t[:, 0:96], adab.t[:], ALU.add, [ps, adab], [modv])
    return modv, modvA


def make_pbank(k, n=6):
    return k.rot(n, [128, 512], F32, "bank", psum=True)


def phase_mod(k, cst, cT, ada_w, adab, out):
    cm = Common(k, cst)
    pbank = make_pbank(k)
    modv, _ = mod_phase(k, cm, cT, ada_w, adab, pbank)
    k.dma("sp", out, modv.t[:], reads=[modv])
    k.end_phase()


class FrontInTail:
    def __init__(self, k, cm, pbank, modv, nmix_in, w_in, gq_in, gkv_in, lat_out, cc):
        self.k, self.cm, self.pbank, self.lat_out, self.cc = k, cm, pbank, lat_out, cc
        nmix = k.sb([128, 16], F32, "nmixf"); k.load(nmix, nmix_in)
        self.gq = k.sb([128, 384], F32, "gq"); k.load(self.gq, gq_in)
        self.gkv = k.sb([128, 256], F32, "gkv"); k.load(self.gkv, gkv_in)
        self.W = k.sb([128, 8, 768], BF16, "Wfront")
        k.dma("pool", self.W.t[:], w_in.rearrange("(k p) n -> p k n", p=128), writes=[self.W])
        ab = k.sb([128, 16], F32, "ab1f")
        k.ts("dve", ab.t[:, 0:8], modv.t[:, 48 + 8:48 + 16], 1.0, None, ALU.add, None, [modv], [ab])
        k.tt("dve", ab.t[:, 0:8], ab.t[:, 0:8], nmix.t[:, 8:16], ALU.mult, [ab, nmix], [ab])
        k.cp("dve", ab.t[:, 8:16], modv.t[:, 48:56], [modv], [ab])
        self.ab = ab
        self.hT_r = k.rot(2, [128, 8, 128], BF16, "hTf")
        self.lat_r = k.rot(2, [128, 768], BF16, "lat")
        self.latT_r = k.rot(2, [128, 6, 128], BF16, "latTs")
        self.l1T = None

    def block(self, xb, blk):
        k, cm, ab = self.k, self.cm, self.ab
        hT = self.hT_r.next()
        cm.norm_hT(xb, xb.t[:], ab, ab.t[:, 0:8], ab.t[:, 8:16], hT, lambda j: hT.t[:, j, :])
        lat = self.lat_r.next()
        p1 = self.pbank.next()
        for kk in range(8):
            k.mm(p1, p1.t[:], hT.t[:, kk, :], self.W.t[:, kk, 0:512], kk == 0, kk == 7, [hT, self.W])
        s1, r1 = cm.rstd_col(p1.t[:, 0:384], p1, 384)
        k.stt(lat.t[:, 0:384], p1.t[:, 0:384], r1, self.gq.t[:], ALU.mult, ALU.mult, [p1, s1, self.gq], [lat])
        k.cp("act", lat.t[:, 384:512], p1.t[:, 384:512], [p1], [lat])
        p2 = self.pbank.next()
        for kk in range(8):
            k.mm(p2, p2.t[:, 0:256], hT.t[:, kk, :], self.W.t[:, kk, 512:768], kk == 0, kk == 7, [hT, self.W])
        s2, r2 = cm.rstd_col(p2.t[:, 0:256], p2, 256)
        k.stt(lat.t[:, 512:768], p2.t[:, 0:256], r2, self.gkv.t[:], ALU.mult, ALU.mult, [p2, s2, self.gkv], [lat])
        pl = cm.pst.next()
        for c in range(6):
            k.tr(pl, pl.t[:, c * 128:(c + 1) * 128], lat.t[:, c * 128:(c + 1) * 128], cm.identb.t[:],
                 [lat, cm.identb])
        lT = self.latT_r.next()
        k.cp("act", lT.t[:], pl.t[:, 0:768].rearrange("p (c t) -> p c t", c=6), [pl], [lT])
        hf = blk // 8
        if blk % 8 == 0:
            self.l1T = [TT(None, f"l1d{c3}_{hf}") for c3 in range(3)]
        tl = (blk % 8) * 128
        for c3 in range(3):
            k.dma("sp", self.lat_out[c3][hf][:, tl:tl + 128].rearrange("(c p) t -> p c t", p=128),
                  lT.t[:, 2 * c3:2 * c3 + 2, :], reads=[lT], writes=[self.l1T[c3]])
        if self.cc is not None and blk % 8 == 7:
            for c3 in range(3):
                cc_allgather(k, self.cc[c3][hf][0], self.cc[c3][hf][1], self.l1T[c3], f"cc2_{c3}_{hf}")


def phase_tail(k, l, final, cst, x_in, G, segmask_in, modv_in, nffn_in, w_out, ffn_in, ffn_out, fn_in, x_out, front=None):
    cm = Common(k, cst)
    pbank = make_pbank(k)
    modv = k.sb([128, 96], F32, "modv")
    k.load(modv, modv_in)
    nffn = k.sb([128, 16], F32, "nffn")
    k.load(nffn, nffn_in)
    segm = k.sb([128, 4], F32, "segm")
    k.load(segm, segmask_in)
    if final:
        fnb = k.sb([128, D], F32, "fnb")
        k.load(fnb, fn_in)
    tail = Tail(k, cm, modv, l, nffn, pbank, lean=front is not None)
    fr = FrontInTail(k, cm, pbank, modv, *front) if front is not None else None
    stg_r = k.rot(2, [128, 8, 256 if front is not None else 512], BF16, "ostg")
    for half in range(2):
        t0 = half * 1024
        xbs = [tail.xb_r.next() for _ in range(8)]
        for blk in range(8):
            k.dma("sp", xbs[blk].t[:], x_in[t0 + blk * 128:t0 + (blk + 1) * 128, :], writes=[xbs[blk]])
        oTs = [tail.oT_r.next(), tail.oT_r.next()]
        SW = 256 if front is not None else 512
        for sub in range(2):
            oT = oTs[sub]
            for piece in range(512 // SW):
                osl = slice(piece * SW, (piece + 1) * SW)
                for s in range(4):
                    stg = stg_r.next()
                    c0 = t0 + sub * 512 + piece * SW
                    G(stg, s, c0, SW)
                    if s == 0:
                        k.ts("dve", oT.t[:, :, osl], stg.t[:], segm.t[:, 0:1], None, ALU.mult, None, [stg, segm], [oT])
                    else:
                        k.stt(oT.t[:, :, osl], stg.t[:], segm.t[:, s:s + 1], oT.t[:, :, osl], ALU.mult, ALU.add,
                              [stg, segm, oT], [oT])
        tail.half(xbs, oTs, w_out, ffn_in, ffn_out)
        for blk in range(8):
            xb = xbs[blk]
            if final:
                s, rs = cm.rstd_col(xb.t[:], xb, 1024)
                k.stt(xb.t[:], xb.t[:], rs, fnb.t[:], ALU.mult, ALU.mult, [xb, s, fnb], [xb])
            k.dma("sp", x_out[t0 + blk * 128:t0 + (blk + 1) * 128, :], xb.t[:], reads=[xb])
            if fr is not None:
                fr.block(xb, half * 8 + blk)
    k.end_phase()


def fm(v, n):
    return np.ascontiguousarray(np.asarray(v, np.float32).reshape(n, 128).T)


def consts():
    c = np.zeros((128, 384), np.float32)
    c[:, 0:128] = np.eye(128, dtype=np.float32)
    c[:, 128:256] = np.triu(np.ones((128, 128), np.float32))
    c[:, 256:384] = 1.0
    return c


def run(nc, in_maps):
    res = run_bass_kernel_spmd(nc, in_maps, core_ids=list(range(len(in_maps))))
    return res.results


def phase_gla(k, cst, x_in, modv_in, nmix_in, w_in, wg_in, bg_in, gout_in, og_out, ngroups=SEQ // 512, cc=None,
              mod_args=None):
    cm = Common(k, cst, npst=1)
    OGT = [TT(None, f"ogd{q}") for q in range(4)]
    pp = k.rot(3 if mod_args else 4, [128, 512], F32, "gpp", psum=True)
    modvA = None
    if mod_args:
        mps = k.rot(1, [128, 512], F32, "modps", psum=True)
        modv_full, modvA = mod_phase(k, cm, mod_args[0], mod_args[1], mod_args[2], mps)
        k.dma("sp", mod_args[3], modv_full.t[:], reads=[modv_full])
    po = [k.ps([128, 512], F32, "po0"), k.ps([128, 512], F32, "po1")]
    pkd = k.ps([128, 512], BF16, "pkd")
    if modvA is None:
        modvA = k.sb([128, 96], F32, "modvg"); k.load(modvA, modv_in)
    modv = modvA
    nmix = k.sb([128, 16], F32, "nmix"); k.load(nmix, nmix_in)
    wg = k.sb([16, 128], F32, "wg"); k.load(wg, wg_in)
    bg = k.sb([1, 128], F32, "bg"); k.load(bg, bg_in)
    gout = k.sb([128, 2], F32, "gout"); k.load(gout, gout_in)
    W = k.sb([128, 8, 784], BF16, "Wgla")
    k.dma("pool", W.t[:], w_in.rearrange("(k p) n -> p k n", p=128), writes=[W])
    ab = k.sb([128, 16], F32, "ab1")
    k.ts("dve", ab.t[:, 0:8], modv.t[:, 8:16], 1.0, None, ALU.add, None, [modv], [ab])
    k.tt("dve", ab.t[:, 0:8], ab.t[:, 0:8], nmix.t[:, 0:8], ALU.mult, [ab, nmix], [ab])
    k.cp("dve", ab.t[:, 8:16], modv.t[:, 0:8], [modv], [ab])
    S = k.sb([128, 256], F32, "S")
    Sb = k.sb([128, 256], BF16, "Sb")
    k.memset("dve", S.t[:], 0.0, [S])
    k.memset("dve", Sb.t[:], 0.0, [Sb])
    ND = 3
    xg_r = k.rot(2, [128, 4, 1024], F32, "xg")
    hT_r = k.rot(2, [128, 8, 512], BF16, "hT")
    qs_r = k.rot(2, [128, 512], F32, "qs")
    ks_r = k.rot(2, [128, 512], F32, "ks")
    glr_r = k.rot(2, [16, 512], F32, "glrT")
    la_r = k.rot(2, [128, 512], F32, "la")
    E1_r = k.rot(ND, [128, 512], F32, "E1")
    E2_r = k.rot(2, [128, 512], F32, "E2")
    qe_r = k.rot(ND, [128, 512], BF16, "qeT")
    ke_r = k.rot(ND, [128, 512], BF16, "keT")
    kdT_r = k.rot(4, [128, 128], BF16, "kdT")
    kd_r = k.rot(ND, [128, 512], BF16, "kd")
    v_r = k.rot(ND, [128, 4, 256], BF16, "vtok")
    e_r = k.rot(2, [128, 2, 512], F32, "er")
    rg_r = k.rot(2, [128, 2, 512], F32, "rg")
    t2_r = k.rot(ND, [128, 2, 512], F32, "t2")
    at_r = k.rot(3, [128, 128], BF16, "attnT")
    os_r = k.rot(2, [128, 2, 512], F32, "osb")
    sq_r = k.rot(2, [128, 2, 512], BF16, "sq")
    onesb = k.sb([128, 128], BF16, "onesbg")
    k.cp("dve", onesb.t[:], cm.ones, [cm.cst], [onesb])
    ms_r = k.rot(2, [128, 512], F32, "ms")
    t1_r = k.rot(2, [128, 512], F32, "t1")
    og_r = k.rot(2, [128, 2, 512], BF16, "og")
    SC = 128 ** -0.5
    mhalf = k.sb([128, 512], F32, "mhalf")
    k.memset("dve", mhalf.t[:], -0.5, [mhalf])
    for g in range(ngroups):
        t0 = g * 512
        xg = xg_r.next()
        k.dma("sp", xg.t[:], x_in[t0:t0 + 512, :].rearrange("(b p) n -> p b n", p=128), writes=[xg])
        hT = hT_r.next()
        for b4 in range(4):
            cm.norm_hT(xg, xg.t[:, b4, :], ab, ab.t[:, 0:8], ab.t[:, 8:16], hT,
                       lambda j, b4=b4: hT.t[:, j, b4 * 128:(b4 + 1) * 128])
        pq = pp.next()
        for kk in range(8):
            k.mm(pq, pq.t[:], W.t[:, kk, 0:128], hT.t[:, kk, :], kk == 0, kk == 7, [W, hT])
        qs = qs_r.next()
        k.cp("act", qs.t[:], pq.t[:], [pq], [qs])
        pk = pp.next()
        for kk in range(8):
            k.mm(pk, pk.t[:], W.t[:, kk, 128:256], hT.t[:, kk, :], kk == 0, kk == 7, [W, hT])
        ks = ks_r.next()
        k.cp("act" if "ks_act" in GLAF else "dve", ks.t[:], pk.t[:], [pk], [ks])
        pg = pp.next()
        for kk in range(8):
            k.mm(pg, pg.t[0:16, :], W.t[:, kk, 768:784], hT.t[:, kk, :], kk == 0, kk == 7, [W, hT])
        glrT = glr_r.next()
        k.cp("act", glrT.t[:], pg.t[0:16, :], [pg], [glrT])
        pxg = pp.next()
        for b4 in range(4):
            k.mm(pxg, pxg.t[:, b4 * 128:(b4 + 1) * 128], glrT.t[:, b4 * 128:(b4 + 1) * 128], wg.t[:], True, False,
                 [glrT, wg])
            k.mm(pxg, pxg.t[:, b4 * 128:(b4 + 1) * 128], cm.cst.t[0:1, 256:384], bg.t[:], False, True,
                 [cm.cst, bg])
        la = la_r.next()
        k.act(la.t[:], pxg.t[:], AF.Exp, [pxg], [la], scale=-1.0)
        k.act(la.t[:], la.t[:], AF.Ln, [la], [la], bias=1.0, scale=1.0)
        pb = pp.next()
        for b4 in range(4):
            k.mm(pb, pb.t[:, b4 * 128:(b4 + 1) * 128], la.t[:, b4 * 128:(b4 + 1) * 128], cm.tri, True, True,
                 [la, cm.cst])
        E1 = E1_r.next()
        E2 = E2_r.next()
        k.act(E1.t[:], pb.t[:], AF.Exp, [pb], [E1], scale=-1.0 / 16)
        k.act(E2.t[:], pb.t[:], AF.Exp, [pb], [E2], scale=1.0 / 16)
        qe = qe_r.next()
        ke = ke_r.next()
        k.stt(qe.t[:], qs.t[:], SC, E1.t[:], ALU.mult, ALU.mult, [qs, E1], [qe])
        k.tt("dve", ke.t[:], ks.t[:], E2.t[:], ALU.mult, [ks, E2], [ke])
        kd = kd_r.next()
        for b4 in range(4):
            kdT = kdT_r.next()
            sl = slice(b4 * 128, (b4 + 1) * 128)
            k.stt(kdT.t[:], ks.t[:, sl], E1.t[:, b4 * 128 + 127:b4 * 128 + 128], E2.t[:, sl], ALU.mult, ALU.mult,
                  [ks, E1, E2], [kdT])
            k.tr(pkd, pkd.t[:, sl], kdT.t[:], cm.identb.t[:], [kdT, cm.identb])
        k.cp("act", kd.t[:], pkd.t[:], [pkd], [kd])
        vt = v_r.next()
        for b2 in range(2):
            pv = pp.next()
            for bb in range(2):
                b4 = b2 * 2 + bb
                for kk in range(8):
                    k.mm(pv, pv.t[:, bb * 256:(bb + 1) * 256], hT.t[:, kk, b4 * 128:(b4 + 1) * 128],
                         W.t[:, kk, 256:512], kk == 0, kk == 7, [hT, W])
            k.cp("act", vt.t[:, b2 * 2:b2 * 2 + 2, :],
                 pv.t[:].rearrange("p (b n) -> p b n", b=2), [pv], [vt])
        er = e_r.next()
        rg = rg_r.next()
        t2 = t2_r.next()
        for c in range(2):
            pr = pp.next()
            for kk in range(8):
                k.mm(pr, pr.t[:], W.t[:, kk, 512 + c * 128:512 + (c + 1) * 128], hT.t[:, kk, :], kk == 0, kk == 7,
                     [W, hT])
            k.act(er.t[:, c, :], pr.t[:], AF.Exp, [pr], [er], scale=-1.0)
            k.act(er.t[:, c, :], er.t[:, c, :], AF.Ln, [er], [er], bias=1.0, scale=1.0)
            k.act(er.t[:, c, :], er.t[:, c, :], AF.Exp, [er], [er], scale=-1.0)
            k.stt(t2.t[:, c, :], pr.t[:], gout.t[:, c:c + 1], er.t[:, c, :], ALU.mult, ALU.mult, [pr, gout, er], [t2])
        for b4 in range(4):
            sl = slice(b4 * 128, (b4 + 1) * 128)
            pa = pp.next()
            k.mm(pa, pa.t[:, 0:128], ke.t[:, sl], qe.t[:, sl], True, True, [ke, qe])
            at = at_r.next()
            k.tt("dve", at.t[:], pa.t[:, 0:128], cm.tri, ALU.mult, [pa, cm.cst], [at])
            for c in range(2):
                k.mm(po[c], po[c].t[:, sl], Sb.t[:, c * 128:(c + 1) * 128], qe.t[:, sl], True, False, [Sb, qe])
                k.mm(po[c], po[c].t[:, sl], vt.t[:, b4, c * 128:(c + 1) * 128], at.t[:], False, True, [vt, at])
            pkv = pp.next()
            k.mm(pkv, pkv.t[:, 0:256], kd.t[:, sl], vt.t[:, b4, :], True, True, [kd, vt])
            k.stt(S.t[:], S.t[:], E1.t[:, b4 * 128 + 127:b4 * 128 + 128], pkv.t[:, 0:256], ALU.mult, ALU.add,
                  [S, E1, pkv], [S])
            k.cp("act", Sb.t[:], S.t[:], [S], [Sb])
        osb = os_r.next()
        sq = sq_r.next()
        for c in range(2):
            if "no_osb" in GLAF:
                k.act(sq.t[:, c, :], po[c].t[:], AF.Square, [po[c]], [sq])
            else:
                k.cp("act", osb.t[:, c, :], po[c].t[:], [po[c]], [osb])
                k.act(sq.t[:, c, :], osb.t[:, c, :], AF.Square, [osb], [sq])
        pss = pp.next()
        for c in range(2):
            k.mm(pss, pss.t[:], onesb.t[:], sq.t[:, c, :], c == 0, c == 1, [onesb, sq])
        ms = ms_r.next()
        k.ts("dve", ms.t[:], pss.t[:], 1.0 / 256, EPS, ALU.mult, ALU.add, [pss], [ms])
        k.act(ms.t[:], ms.t[:], AF.Ln, [ms], [ms])
        k.act(ms.t[:], ms.t[:], AF.Exp, [ms], [ms], scale=-0.5)
        og = og_r.next()
        for c in range(2):
            t1 = t1_r.next()
            if "no_osb" in GLAF:
                k.tt("dve", t1.t[:], po[c].t[:], ms.t[:], ALU.mult, [po[c], ms], [t1])
            else:
                k.tt("dve", t1.t[:], osb.t[:, c, :], ms.t[:], ALU.mult, [osb, ms], [t1])
            k.tt("dve", og.t[:, c, :], t1.t[:], t2.t[:, c, :], ALU.mult, [t1, t2], [og])
        k.dma("sp", og_out[g // 4][:, (g % 4) * 512:(g % 4) * 512 + 512].rearrange("(c p) t -> p c t", p=128), og.t[:],
              reads=[og], writes=[OGT[g // 4]])
        if cc is not None and g % 4 == 3:
            cc_allgather(k, cc[g // 4][0], cc[g // 4][1], OGT[g // 4], f"cc1_{g // 4}")
    k.end_phase()


def phase_front(k, cst, x_in, modv_in, nmix_in, w_in, gq_in, gkv_in, lat_out, cc=None):
    cm = Common(k, cst, npst=2)
    pb1 = k.rot(2, [128, 512], F32, "fb1", psum=True)
    pb2 = k.rot(2, [128, 512], F32, "fb2", psum=True)
    plt = k.rot(2, [128, 1024], BF16, "flt", psum=True)
    modv = k.sb([128, 96], F32, "modv"); k.load(modv, modv_in)
    nmix = k.sb([128, 16], F32, "nmix"); k.load(nmix, nmix_in)
    gq = k.sb([128, 384], F32, "gq"); k.load(gq, gq_in)
    gkv = k.sb([128, 256], F32, "gkv"); k.load(gkv, gkv_in)
    W = k.sb([128, 8, 768], BF16, "Wfront")
    k.dma("pool", W.t[:], w_in.rearrange("(k p) n -> p k n", p=128), writes=[W])
    ab = k.sb([128, 16], F32, "ab1")
    k.ts("dve", ab.t[:, 0:8], modv.t[:, 48 + 8:48 + 16], 1.0, None, ALU.add, None, [modv], [ab])
    k.tt("dve", ab.t[:, 0:8], ab.t[:, 0:8], nmix.t[:, 8:16], ALU.mult, [ab, nmix], [ab])
    k.cp("dve", ab.t[:, 8:16], modv.t[:, 48:56], [modv], [ab])
    xb_r = k.rot(4, [128, 1024], F32, "xb")
    hT_r = k.rot(4, [128, 8, 128], BF16, "hTf")
    lat_r = k.rot(4, [128, 768], BF16, "lat")
    latT_r = k.rot(4, [128, 6, 128], BF16, "latTs")
    for blk in range(NTOK // 128):
        t0 = blk * 128
        xb = xb_r.next()
        k.dma("sp", xb.t[:], x_in[t0:t0 + 128, :], writes=[xb])
        hT = hT_r.next()
        cm.norm_hT(xb, xb.t[:], ab, ab.t[:, 0:8], ab.t[:, 8:16], hT, lambda j: hT.t[:, j, :])
        p1 = pb1.next()
        p2 = pb2.next()
        for kk in range(8):
            k.mm(p1, p1.t[:], hT.t[:, kk, :], W.t[:, kk, 0:512], kk == 0, kk == 7, [hT, W])
        for kk in range(8):
            k.mm(p2, p2.t[:, 0:256], hT.t[:, kk, :], W.t[:, kk, 512:768], kk == 0, kk == 7, [hT, W])
        lat = lat_r.next()
        s1, r1 = cm.rstd_col(p1.t[:, 0:384], p1, 384)
        k.stt(lat.t[:, 0:384], p1.t[:, 0:384], r1, gq.t[:], ALU.mult, ALU.mult, [p1, s1, gq], [lat])
        k.cp("act", lat.t[:, 384:512], p1.t[:, 384:512], [p1], [lat])
        s2, r2 = cm.rstd_col(p2.t[:, 0:256], p2, 256)
        k.stt(lat.t[:, 512:768], p2.t[:, 0:256], r2, gkv.t[:], ALU.mult, ALU.mult, [p2, s2, gkv], [lat])
        pl = plt.next()
        for c in range(6):
            k.tr(pl, pl.t[:, c * 128:(c + 1) * 128], lat.t[:, c * 128:(c + 1) * 128], cm.identb.t[:],
                 [lat, cm.identb])
        lT = latT_r.next()
        k.cp("act", lT.t[:], pl.t[:, 0:768].rearrange("p (c t) -> p c t", c=6), [pl], [lT])
        hf = blk // 8
        if blk % 8 == 0:
            l1T = [TT(None, f"l1d{c3}_{hf}") for c3 in range(3)]
        tl = (blk % 8) * 128
        for c3 in range(3):
            k.dma("sp", lat_out[c3][hf][:, tl:tl + 128].rearrange("(c p) t -> p c t", p=128),
                  lT.t[:, 2 * c3:2 * c3 + 2, :], reads=[lT], writes=[l1T[c3]])
        if cc is not None and blk % 8 == 7:
            for c3 in range(3):
                cc_allgather(k, cc[c3][hf][0], cc[c3][hf][1], l1T[c3], f"cc2_{c3}_{hf}")
    k.end_phase()


TWO_PI = 2.0 * np.pi
C1 = 6.28125
C2 = TWO_PI - 6.28125


def phase_attn(k, cst, latG, wq_in, wkv_in, pos_in, ropec_in, dmask_in, o_out, cc=None):
    def lat(f0, f1, t0, t1):
        s = t0 // NTOK
        c = f0 // 256
        hf = (t0 % NTOK) // 1024
        assert (t1 - 1) // NTOK == s and (f1 - 1) // 256 == c and ((t1 - 1) % NTOK) // 1024 == hf
        tl = t0 - s * NTOK - hf * 1024
        return latG[c][hf][s * 256 + f0 - c * 256:s * 256 + f1 - c * 256, tl:tl + (t1 - t0)]

    cm = Common(k, cst, npst=0, norm=False)
    psS = k.rot(4, [128, 512], F32, "psS", psum=True)
    psO = k.rot(1, [128, 512], F32, "psO", psum=True)
    psL = k.rot(1, [128, 512], F32, "psL", psum=True)
    pj = k.rot(2, [128, 512], F32, "pj", psum=True)
    onesb = k.sb([128, 128], BF16, "onesb")
    k.cp("dve", onesb.t[:], cm.ones, [cm.cst], [onesb])
    ropec = k.sb([64, 2], F32, "ropec"); k.load(ropec, ropec_in)
    dmask = k.sb([128, 4, 512], BF16, "dmask")
    k.dma("pool", dmask.t[:], dmask_in, writes=[dmask])
    wq = k.sb([128, 3, 512], BF16, "wq")
    k.dma("pool", wq.t[:], wq_in.rearrange("(c p) n -> p c n", p=128), writes=[wq])
    wkv = k.sb([128, 2, 512], BF16, "wkv")
    k.dma("pool", wkv.t[:], wkv_in.rearrange("(c p) n -> p c n", p=128), writes=[wkv])
    ckv = k.sb([128, 2, SEQ], BF16, "ckvT")
    ckvr = [[TT(ckv.t, f"ckvT{c}_{r}") for r in range(8)] for c in range(2)]
    for r in range(8):
        for c in range(2):
            k.dma("sp", ckv.t[:, c, r * 1024:(r + 1) * 1024],
                  lat(512 + c * 128, 512 + (c + 1) * 128, r * 1024, (r + 1) * 1024), writes=[ckvr[c][r]])
    cosT = k.sb([64, SEQ], BF16, "cosT")
    sinT = k.sb([64, SEQ], BF16, "sinT")
    krot = k.sb([64, SEQ], BF16, "krot")
    cosR = [TT(cosT.t, f"cosT_{r}") for r in range(8)]
    sinR = [TT(sinT.t, f"sinT_{r}") for r in range(8)]
    krotR = [TT(krot.t, f"krot_{r}") for r in range(8)]
    CH = 1024
    posi_r = k.rot(2, [64, CH], I32, "posi")
    f_r = k.rot(6, [64, CH], F32, "ropef")
    ti_r = k.rot(2, [64, CH], I32, "ropei")
    kr_r = k.rot(2, [64, 2, CH], BF16, "krin")
    for ch in range(SEQ // CH):
        sl = slice(ch * CH, (ch + 1) * CH)
        pi_ = posi_r.next()
        k.dma("sp", pi_.t[:], pos_in[:, sl], writes=[pi_])
        ang = f_r.next()
        k.cp("dve", ang.t[:], pi_.t[:], [pi_], [ang])
        k.ts("dve", ang.t[:], ang.t[:], ropec.t[:, 0:1], None, ALU.mult, None, [ang, ropec], [ang])
        for which, dst in ((0, sinR[ch]), (1, cosR[ch])):
            a2 = ang
            if which == 1:
                a2 = f_r.next()
                k.ts("dve", a2.t[:], ang.t[:], float(np.pi / 2), None, ALU.add, None, [ang], [a2])
            ti = ti_r.next()
            k.ts("dve", ti.t[:], a2.t[:], float(1.0 / TWO_PI), None, ALU.mult, None, [a2], [ti])
            kf = f_r.next()
            k.cp("dve", kf.t[:], ti.t[:], [ti], [kf])
            r = f_r.next()
            k.stt(r.t[:], kf.t[:], -C1, a2.t[:], ALU.mult, ALU.add, [kf, a2], [r])
            k.stt(r.t[:], kf.t[:], -C2, r.t[:], ALU.mult, ALU.add, [kf, r], [r])
            k.ts("dve", r.t[:], r.t[:], float(-np.pi), float(np.pi), ALU.max, ALU.min, [r], [r])
            if which == 0:
                k.act(r.t[:], r.t[:], AF.Sin, [r], [r])
                k.ts("dve", dst.t[:, sl], r.t[:], ropec.t[:, 1:2], None, ALU.mult, None, [r, ropec], [dst])
            else:
                k.act(dst.t[:, sl], r.t[:], AF.Sin, [r], [dst])
        kr = kr_r.next()
        k.dma("sp", kr.t[:], lat(384, 512, sl.start, sl.stop).rearrange("(c p) t -> p c t", p=64), writes=[kr])
        t1 = f_r.next()
        t2 = f_r.next()
        k.tt("dve", t1.t[:], kr.t[:, 0, :], cosT.t[:, sl], ALU.mult, [kr, cosR[ch]], [t1])
        k.tt("dve", t2.t[:], kr.t[:, 1, :], sinT.t[:, sl], ALU.mult, [kr, sinR[ch]], [t2])
        k.tt("dve", krot.t[:, sl], t1.t[:], t2.t[:], ALU.add, [t1, t2], [krotR[ch]])
    KT = k.sb([128, SEQ], BF16, "KT")
    KTR = [TT(KT.t, f"KT_{t}") for t in range(16)]
    V = k.sb([128, 64, 128], BF16, "Vtok")
    VR = [TT(V.t, f"V_{t}") for t in range(16)]
    cq_r = k.rot(2, [128, 3, 512], BF16, "cqT")
    QnT_r = k.rot(2, [128, 512], BF16, "QnT")
    Qrot_r = k.rot(2, [64, 512], BF16, "Qrot")
    qt_r = k.rot(4, [64, 512], F32, "qtmp")
    PT_r = k.rot(5, [128, 512], BF16, "PT")
    rl_r = k.rot(2, [128, 512], F32, "rl")
    ot_r = k.rot(2, [128, 512], BF16, "ot")
    SCALE = float(192 ** -0.5)
    for hh in range(2):
        for t in range(16):
            p = pj.next()
            sl = slice(t * 512, (t + 1) * 512)
            for c in range(2):
                k.mm(p, p.t[:], wkv.t[:, c, hh * 256:hh * 256 + 128], ckv.t[:, c, sl], c == 0, c == 1,
                     [wkv, ckvr[c][t // 2]])
            k.cp("act", KT.t[:, sl], p.t[:], [p], [KTR[t]])
        for t in range(16):
            p = pj.next()
            for b4 in range(4):
                kb = t * 4 + b4
                for c in range(2):
                    k.mm(p, p.t[:, b4 * 128:(b4 + 1) * 128], ckv.t[:, c, kb * 128:(kb + 1) * 128],
                         wkv.t[:, c, hh * 256 + 128:hh * 256 + 256], c == 0, c == 1, [ckvr[c][t // 2], wkv])
            k.cp("dve", V.t[:, t * 4:(t + 1) * 4, :], p.t[:].rearrange("p (b n) -> p b n", b=4), [p], [VR[t]])
        for i in range(16):
            sl = slice(i * 512, (i + 1) * 512)
            cq = cq_r.next()
            k.dma("sp", cq.t[:, 0:2, :], lat(0, 256, sl.start, sl.stop).rearrange("(c p) t -> p c t", p=128), writes=[cq])
            k.dma("sp", cq.t[:, 2, :], lat(256, 384, sl.start, sl.stop), writes=[cq])
            p = pj.next()
            for c in range(3):
                k.mm(p, p.t[:], wq.t[:, c, hh * 256:hh * 256 + 128], cq.t[:, c, :], c == 0, c == 2, [wq, cq])
            QnT = QnT_r.next()
            k.cp("act", QnT.t[:], p.t[:], [p], [QnT])
            p1 = pj.next()
            for c in range(3):
                k.mm(p1, p1.t[0:64, :], wq.t[:, c, hh * 256 + 128:hh * 256 + 192], cq.t[:, c, :], c == 0, c == 2,
                     [wq, cq])
            p2 = pj.next()
            for c in range(3):
                k.mm(p2, p2.t[0:64, :], wq.t[:, c, hh * 256 + 192:hh * 256 + 256], cq.t[:, c, :], c == 0, c == 2,
                     [wq, cq])
            q1 = qt_r.next()
            q2 = qt_r.next()
            k.tt("dve", q1.t[:], p1.t[0:64, :], cosT.t[:, sl], ALU.mult, [p1, cosR[i // 2]], [q1])
            k.tt("dve", q2.t[:], p2.t[0:64, :], sinT.t[:, sl], ALU.mult, [p2, sinR[i // 2]], [q2])
            Qrot = Qrot_r.next()
            k.tt("dve", Qrot.t[:], q1.t[:], q2.t[:], ALU.add, [q1, q2], [Qrot])
            pO = psO.next()
            pL = psL.next()
            nkb = 4 * i + 4
            for kb in range(nkb):
                ks = slice(kb * 128, (kb + 1) * 128)
                pS = psS.next()
                k.mm(pS, pS.t[:], KT.t[:, ks], QnT.t[:], True, False, [KTR[kb // 4], QnT])
                k.mm(pS, pS.t[:], krot.t[:, ks], Qrot.t[:], False, True, [krotR[kb // 8], Qrot])
                PT = PT_r.next()
                k.act(PT.t[:], pS.t[:], AF.Exp, [pS], [PT], scale=SCALE)
                if kb >= 4 * i:
                    k.tt("dve", PT.t[:], PT.t[:], dmask.t[:, kb - 4 * i, :], ALU.mult, [PT, dmask], [PT])
                k.mm(pO, pO.t[:], V.t[:, kb, :], PT.t[:], kb == 0, kb == nkb - 1, [VR[kb // 4], PT])
                k.mm(pL, pL.t[:], onesb.t[:], PT.t[:], kb == 0, kb == nkb - 1, [onesb, PT])
            rl = rl_r.next()
            k.recip(rl.t[:], pL.t[:], [pL], [rl])
            ot = ot_r.next()
            k.tt("dve", ot.t[:], pO.t[:], rl.t[:], ALU.mult, [pO, rl], [ot])
            if i % 4 == 0:
                oaT = TT(None, f"oad{hh}_{i // 4}")
            k.dma("sp", o_out[hh][i // 4][:, (i % 4) * 512:(i % 4) * 512 + 512], ot.t[:], reads=[ot], writes=[oaT])
            if cc is not None and i % 4 == 3:
                cc_allgather(k, cc[hh][i // 4][0], cc[hh][i // 4][1], oaT, f"cc3_{hh}_{i // 4}")
    k.end_phase()


GROUPS = [[0, 1, 2, 3], [4, 5, 6, 7]]


def cc_allgather(k, src_t, dst_t, srcTT, key):
    k.P.op("pool", lambda e: e.collective_compute("AllGather", ALU.bypass, replica_groups=GROUPS,
                                                  ins=[src_t.ap().opt()], outs=[dst_t.ap().opt()]),
           [srcTT.b], [], dma_key=key, inc=1, cost=60.0)


def allgather(k, pairs, key):
    for i, (src_t, dst_t) in enumerate(pairs):
        k.P.op("pool", lambda e, src_t=src_t, dst_t=dst_t: e.collective_compute(
            "AllGather", ALU.bypass, replica_groups=GROUPS, ins=[src_t.ap().opt()], outs=[dst_t.ap().opt()]),
            [], [], dma_key=f"{key}_{i}", inc=1)
    k.end_phase()


def build_fused(upto=99):
    nc = bass.Bass("TRN2", target_bir_lowering=False)
    k = KB(nc)
    cst = k.din("cst", [128, 384]); cT = k.din("cT", [128, 8]); ada_w = k.din("ada_w", [2, D, 6 * D])
    adab = k.din("adab", [128, 96]); nmix = k.din("nmix", [128, 16]); nffn = k.din("nffn", [128, 16])
    segm = k.din("segm", [128, 4]); xfull = k.din("xfull", [SEQ, D]); xseg = k.din("xseg", [NTOK, D])
    gw_in = k.din("gw_in", [D, 784]); gw_gate = k.din("gw_gate", [16, 128]); gb_gate = k.din("gb_gate", [1, 128])
    g_out = k.din("g_out", [128, 2]); gla_w_out = k.din("gla_w_out", [D, D])
    ffn_in0 = k.din("ffn_in0", [D, 2 * DFF]); ffn_out0 = k.din("ffn_out0", [DFF, D])
    wfront = k.din("wfront", [D, 768]); gq = k.din("gq", [128, 384]); gkv = k.din("gkv", [128, 256])
    wq = k.din("wq", [384, 512]); wkv = k.din("wkv", [256, 512]); pos = k.din("pos", [64, SEQ], I32)
    ropec = k.din("ropec", [64, 2]); dmask = k.din("dmask", [128, 4, 512])
    mla_w_out = k.din("mla_w_out", [D, D]); ffn_in1 = k.din("ffn_in1", [D, 2 * DFF])
    ffn_out1 = k.din("ffn_out1", [DFF, D]); fnorm = k.din("fnorm", [128, D])
    out = k.dout("out", [NTOK, D])
    MODV = k.dint("i_modv", [128, 96])
    X0 = k.dint("i_x0", [NTOK, D])
    OG = [k.dint(f"i_og{q}", [256, NTOK], BF16) for q in range(4)]
    G1 = [k.dint(f"i_g1{q}", [1024, NTOK], BF16) for q in range(4)]
    L1 = [[k.dint(f"i_l1{c}_{h}", [256, 1024], BF16) for h in range(2)] for c in range(3)]
    G2 = [[k.dint(f"i_g2{c}_{h}", [1024, 1024], BF16) for h in range(2)] for c in range(3)]
    OA = [[k.dint(f"i_oa{hh}_{q}", [128, NTOK], BF16) for q in range(4)] for hh in range(2)]
    G3 = [[k.dint(f"i_g3{hh}_{q}", [512, NTOK], BF16) for q in range(4)] for hh in range(2)]
    aps = lambda ts: [t.ap() for t in ts]

    def g1_load(stg, s, c0, w=512):
        k.dma("sp", stg.t[:], G1[s].ap()[:, c0:c0 + w].rearrange("(c p) t -> p c t", p=128), writes=[stg])

    def g3_load(stg, s, c0, w=512):
        for hh in range(2):
            k.dma("sp", stg.t[:, hh:8:2, :], G3[hh][s].ap()[:, c0:c0 + w].rearrange("(c p) t -> p c t", p=128),
                  writes=[stg])

    phase_gla(k, cst, xfull, None, nmix, gw_in, gw_gate, gb_gate, g_out, aps(OG), cc=list(zip(OG, G1)),
              mod_args=(cT, ada_w, adab, MODV.ap()))
    if upto <= 2:
        return k.finish(), k
    phase_tail(k, 0, False, cst, xseg, g1_load, segm, MODV.ap(), nffn, gla_w_out, ffn_in0, ffn_out0, None, X0.ap(),
               front=(nmix, wfront, gq, gkv, [aps(L1[c]) for c in range(3)],
                      [[(L1[c][h], G2[c][h]) for h in range(2)] for c in range(3)]))
    if upto <= 3:
        return k.finish(), k
    phase_attn(k, cst, [aps(G2[c]) for c in range(3)], wq, wkv, pos, ropec, dmask, [aps(OA[0]), aps(OA[1])],
               cc=[list(zip(OA[0], G3[0])), list(zip(OA[1], G3[1]))])
    if upto <= 5:
        return k.finish(), k
    phase_tail(k, 1, True, cst, X0.ap(), g3_load, segm, MODV.ap(), nffn, mla_w_out, ffn_in1, ffn_out1, fnorm, out)
    return k.finish(), k
    phase_gla(k, cst, xfull, MODV.ap(), nmix, gw_in, gw_gate, gb_gate, g_out, aps(OG))
    if upto <= 2:
        return k.finish(), k
    allgather(k, list(zip(OG, G1)), "cc1")
    if upto <= 3:
        return k.finish(), k
    phase_tail(k, 0, False, cst, xseg, aps(G1), segm, MODV.ap(), nffn, gla_w_out, ffn_in0, ffn_out0, None, X0.ap())
    if upto <= 4:
        return k.finish(), k
    phase_front(k, cst, X0.ap(), MODV.ap(), nmix, wfront, gq, gkv, aps(L1))
    if upto <= 5:
        return k.finish(), k
    allgather(k, list(zip(L1, G2)), "cc2")
    if upto <= 6:
        return k.finish(), k
    phase_attn(k, cst, aps(G2), wq, wkv, pos, ropec, dmask, aps(OA))
    if upto <= 7:
        return k.finish(), k
    allgather(k, list(zip(OA, G3)), "cc3")
    if upto <= 8:
        return k.finish(), k
    phase_tail(k, 1, True, cst, X0.ap(), aps(G3), segm, MODV.ap(), nffn, mla_w_out, ffn_in1, ffn_out1, fnorm, out)
    if upto <= 9:
        return k.finish(), k
    return k.finish(), k


_CACHE = {}
_PREP_ONLY = False


def _swap_pairs(w):
    idx = np.arange(w.shape[1]).reshape(-1, 2)[:, ::-1].reshape(-1)
    return w[:, idx]


def kernel(x, c, positions, ada_w, ada_b, norm_mix, norm_ffn,
           gla_w_in, gla_w_gate, gla_b_gate, gla_g_out, gla_w_out,
           mla_w_in, mla_g_q, mla_w_q_up, mla_g_kv, mla_w_kv_up, mla_w_out,
           ffn_w_in, ffn_w_out, final_norm):
    f32 = np.float32
    A = lambda v: np.asarray(v, f32)
    x = A(x); c = A(c); positions = np.asarray(positions, np.int32)
    ada_w = A(ada_w); ada_b = A(ada_b)
    cst = consts()
    nmix = np.concatenate([fm(A(norm_mix)[l], 8) for l in range(2)], axis=1)
    nffn = np.concatenate([fm(A(norm_ffn)[l], 8) for l in range(2)], axis=1)
    adab = np.concatenate([fm(ada_b[l], 48) for l in range(2)], axis=1)
    Wi = A(gla_w_in)[0]; wg = A(gla_w_gate)[0]; bgt = A(gla_b_gate)[0]
    gout = fm(A(gla_g_out)[0], 2)
    Wm = A(mla_w_in)[0]
    wfront = np.ascontiguousarray(np.concatenate([Wm[:, 0:384], Wm[:, 640:704], _swap_pairs(Wm[:, 640:704]),
                                                  Wm[:, 384:640]], axis=1))
    gq = np.ascontiguousarray(np.broadcast_to(A(mla_g_q)[0][None, :], (128, 384)))
    gkv = np.ascontiguousarray(np.broadcast_to(A(mla_g_kv)[0][None, :], (128, 256)))
    Wq = A(mla_w_q_up)[0]; Wkv = A(mla_w_kv_up)[0]
    inv_freq = (np.float32(10000.0) ** (-np.arange(0, 64, 2, dtype=np.float32) / np.float32(64))).astype(f32)
    ropec = np.stack([np.repeat(inv_freq, 2), np.tile(np.array([-1.0, 1.0], f32), 32)], axis=1).astype(f32)
    pidx = np.arange(128)[:, None, None] + 128 * np.arange(4)[None, :, None]
    dmask = (pidx <= np.arange(512)[None, None, :]).astype(f32)
    fnb = np.ascontiguousarray(np.broadcast_to(A(final_norm)[None, :], (128, D)))
    ims = []
    for cr in range(8):
        b, j = cr // 4, cr % 4
        w = np.concatenate([Wi[:, j * 128:(j + 1) * 128], Wi[:, 512 + j * 128:512 + (j + 1) * 128],
                            Wi[:, 1024 + j * 256:1024 + (j + 1) * 256], Wi[:, 2048 + j * 256:2048 + (j + 1) * 256],
                            Wi[:, 3072:3088]], axis=1)
        wq_cols, wkv_cols = [], []
        for h in (2 * j, 2 * j + 1):
            wq_cols += [Wq[:, h * 192:h * 192 + 128], Wq[:, h * 192 + 128:h * 192 + 192],
                        _swap_pairs(Wq[:, h * 192 + 128:h * 192 + 192])]
            wkv_cols += [Wkv[:, h * 256:h * 256 + 128], Wkv[:, h * 256 + 128:h * 256 + 256]]
        segm = np.zeros((128, 4), f32)
        segm[:, j] = 1.0
        ims.append({
            "cst": cst, "cT": fm(c[b], 8), "ada_w": ada_w, "adab": adab, "nmix": nmix, "nffn": nffn, "segm": segm,
            "xfull": x[b], "xseg": np.ascontiguousarray(x[b, j * NTOK:(j + 1) * NTOK]),
            "gw_in": np.ascontiguousarray(w), "gw_gate": np.ascontiguousarray(wg[:, j * 128:(j + 1) * 128]),
            "gb_gate": np.ascontiguousarray(bgt[None, j * 128:(j + 1) * 128]), "g_out": gout,
            "gla_w_out": A(gla_w_out)[0], "ffn_in0": A(ffn_w_in)[0], "ffn_out0": A(ffn_w_out)[0],
            "wfront": wfront, "gq": gq, "gkv": gkv,
            "wq": np.ascontiguousarray(np.concatenate(wq_cols, axis=1)),
            "wkv": np.ascontiguousarray(np.concatenate(wkv_cols, axis=1)),
            "pos": np.ascontiguousarray(np.broadcast_to(positions[b][None, :], (64, SEQ))),
            "ropec": ropec, "dmask": dmask, "mla_w_out": A(mla_w_out)[0], "ffn_in1": A(ffn_w_in)[1],
            "ffn_out1": A(ffn_w_out)[1], "fnorm": fnb})
    if _PREP_ONLY:
        return ims
    if "nc" not in _CACHE:
        _CACHE["nc"] = build_fused()[0]
    res = run_bass_kernel_spmd(_CACHE["nc"], ims, core_ids=list(range(8))).results
    out = np.stack([np.concatenate([np.asarray(res[b * 4 + s]["out"]) for s in range(4)], axis=0)
                    for b in range(2)])
    return out.astype(f32)
```
